# Optimizing a Trainium2 kernel written in Bass

```python
import math
import jax, jax.numpy as jnp
from jax import lax
import numpy as np

D_MODEL = 1024
BATCH = 8
SEQ = 2048
DEPTH = 1
DEC_BATCH = 128
DEC_SEQ = 4
PAST_LEN = 16384
PAGE_SIZE = 128

MIX_WIDTH = D_MODEL
HGRN_WIDTH = MIX_WIDTH // 2
RET_WIDTH = MIX_WIDTH - HGRN_WIDTH
HGRN_HEADS = 4
HGRN_HEAD_DIM = HGRN_WIDTH // HGRN_HEADS
RET_HEADS = 4
RET_HEAD_DIM = RET_WIDTH // RET_HEADS
IN_SIZES = [HGRN_WIDTH] * 4 + [RET_WIDTH] * 4
IN_COLS = sum(IN_SIZES)
D_FF = ((8 * D_MODEL // 3 + 255) // 256) * 256
CHUNK = 64
ROPE_BASE = 10000.0
DEEPNORM_ALPHA = (2.0 * DEPTH) ** 0.25
DEEPNORM_BETA = (8.0 * DEPTH) ** -0.25
EPS = 1e-5

kernel_name = "hymba_hgrn2_retnet_deepnorm_step"

F32 = jnp.float32


def _layer_norm(x, g, b):
    xf = x.astype(F32)
    mu = jnp.mean(xf, -1, keepdims=True)
    var = jnp.mean(jnp.square(xf - mu), -1, keepdims=True)
    return ((xf - mu) * lax.rsqrt(var + EPS) * g.astype(F32) + b.astype(F32)).astype(x.dtype)


def _head_rmsnorm(o, g):
    return o * lax.rsqrt(jnp.mean(jnp.square(o), -1, keepdims=True) + EPS) * g.astype(F32)


def _rotary(x, pos):
    half = x.shape[-1] // 2
    inv = ROPE_BASE ** (-jnp.arange(half, dtype=F32) / half)
    ang = pos[:, None] * inv[None, :]
    cos = jnp.cos(ang)[None, :, None, :]
    sin = jnp.sin(ang)[None, :, None, :]
    x1, x2 = x[..., :half], x[..., half:]
    return jnp.concatenate([x1 * cos - x2 * sin, x1 * sin + x2 * cos], axis=-1)


def _to_chunks(x, c):
    B, T, H, D = x.shape
    return x.reshape(B, T // c, c, H, D).transpose(1, 0, 3, 2, 4)


def _from_chunks(x):
    n, B, H, c, D = x.shape
    return x.transpose(1, 0, 3, 2, 4).reshape(B, n * c, H, D)


def _hgrn2_chunkwise(q, k, v, logf, s0):
    T = q.shape[1]
    c = math.gcd(T, CHUNK)
    m5 = jnp.tril(jnp.ones((c, c), dtype=bool))[:, :, None]

    def step(S, inp):
        qc, kc, vc, gc = inp
        b = jnp.cumsum(gc, axis=2)
        diff = b[:, :, :, None, :] - b[:, :, None, :, :]
        decay = jnp.exp(jnp.where(m5, diff, -jnp.inf))
        A = jnp.einsum('bhid,bhjd,bhijd->bhij', qc, kc, decay)
        o = (jnp.einsum('bhij,bhjv->bhiv', A, vc)
             + jnp.einsum('bhid,bhdv->bhiv', qc * jnp.exp(b), S))
        b_last = b[:, :, -1:, :]
        S_new = (jnp.exp(b_last[:, :, 0, :])[..., None] * S
                 + jnp.einsum('bhjd,bhjv->bhdv', kc * jnp.exp(b_last - b), vc))
        return S_new, o

    S, o = lax.scan(step, s0, (_to_chunks(q, c), _to_chunks(k, c), _to_chunks(v, c), _to_chunks(logf, c)))
    return _from_chunks(o), S


def _retention_chunkwise(q, k, v, s0):
    T, H = q.shape[1], q.shape[2]
    c = math.gcd(T, CHUNK)
    lg = jnp.log1p(-jnp.exp2(-5.0 - jnp.arange(H, dtype=F32)))
    idx = jnp.arange(c, dtype=F32)
    mask = jnp.tril(jnp.ones((c, c), dtype=bool))
    rel = jnp.where(mask, idx[:, None] - idx[None, :], 0.0)
    dmat = jnp.where(mask[None], jnp.exp(lg[:, None, None] * rel[None]), 0.0)
    cross = jnp.exp(lg[:, None] * (idx + 1.0)[None, :])
    upd = jnp.exp(lg[:, None] * (c - 1.0 - idx)[None, :])
    chunk_dec = jnp.exp(lg * c)

    def step(S, inp):
        qc, kc, vc = inp
        A = jnp.einsum('bhid,bhjd->bhij', qc, kc) * dmat[None]
        o = (jnp.einsum('bhij,bhjv->bhiv', A, vc)
             + jnp.einsum('bhid,bhdv->bhiv', qc, S) * cross[None, :, :, None])
        S_new = (chunk_dec[None, :, None, None] * S
                 + jnp.einsum('bhjd,bhjv->bhdv', kc * upd[None, :, :, None], vc))
        return S_new, o

    S, o = lax.scan(step, s0, (_to_chunks(q, c), _to_chunks(k, c), _to_chunks(v, c)))
    return _from_chunks(o), S


def _mixer(x, pos, s_hgrn, s_ret, w_in, lb, g_hgrn, g_ret, w_out):
    B, T, _ = x.shape
    proj = jnp.einsum('btd,dc->btc', x, w_in).astype(F32)
    hq, hf, hi, hg, rq, rk, rv, rg = jnp.split(proj, list(np.cumsum(IN_SIZES)[:-1]), axis=-1)

    def heads(t, h):
        return t.reshape(B, T, h, -1)

    f = lb + (1.0 - lb) * jax.nn.sigmoid(hf)
    logf = jnp.log(f)
    k_h = (1.0 - lb) * jax.nn.sigmoid(-hf)
    o_h, s_h = _hgrn2_chunkwise(heads(jax.nn.silu(hq), HGRN_HEADS), heads(k_h, HGRN_HEADS),
                                heads(hi, HGRN_HEADS), heads(logf, HGRN_HEADS), s_hgrn.astype(F32))
    o_h = _head_rmsnorm(o_h, g_hgrn) * heads(jax.nn.silu(hg), HGRN_HEADS)

    q_r = _rotary(heads(rq, RET_HEADS), pos)
    k_r = _rotary(heads(rk, RET_HEADS), pos) * (RET_HEAD_DIM ** -0.5)
    o_r, s_r = _retention_chunkwise(q_r, k_r, heads(rv, RET_HEADS), s_ret.astype(F32))
    o_r = _head_rmsnorm(o_r, g_ret) * heads(jax.nn.silu(rg), RET_HEADS)

    mix = jnp.concatenate([o_h.reshape(B, T, HGRN_WIDTH), o_r.reshape(B, T, RET_WIDTH)], axis=-1)
    out = jnp.einsum('btc,cd->btd', mix.astype(x.dtype), w_out)
    return out, s_h, s_r


def _trunk(x, pos, state_hgrn, state_ret, w_in, lb_logits, hgrn_norm_g, ret_norm_g, w_out,
           ln1_g, ln1_b, w_gate, w_up, w_down, ln2_g, ln2_b):
    lb_all = jnp.cumsum(jax.nn.softmax(lb_logits.astype(F32), axis=0), axis=0)
    new_h, new_r = [], []
    for l in range(DEPTH):
        m, sh, sr = _mixer(x, pos, state_hgrn[l], state_ret[l], w_in[l], lb_all[l],
                           hgrn_norm_g[l], ret_norm_g[l], w_out[l])
        x = _layer_norm(DEEPNORM_ALPHA * x + m, ln1_g[l], ln1_b[l])
        hidden = jax.nn.silu(jnp.einsum('btd,df->btf', x, w_gate[l])) * jnp.einsum('btd,df->btf', x, w_up[l])
        x = _layer_norm(DEEPNORM_ALPHA * x + jnp.einsum('btf,fd->btd', hidden, w_down[l]), ln2_g[l], ln2_b[l])
        new_h.append(sh.astype(state_hgrn.dtype))
        new_r.append(sr.astype(state_ret.dtype))
    return x, jnp.stack(new_h), jnp.stack(new_r)


def setup_inputs(seed: int = 0) -> dict:
    key = jax.random.key(seed)
    ks = jax.random.split(key, 20)
    col_scale = np.ones((IN_COLS,), np.float32)
    offs = np.cumsum([0] + IN_SIZES)
    col_scale[offs[2]:offs[3]] = DEEPNORM_BETA
    col_scale[offs[6]:offs[7]] = DEEPNORM_BETA
    w_in = jax.random.normal(ks[0], (DEPTH, D_MODEL, IN_COLS), F32) * (D_MODEL ** -0.5) * jnp.asarray(col_scale)
    return {
        "x_prompt": jax.random.normal(ks[1], (BATCH, SEQ, D_MODEL), F32),
        "x_sample": jax.random.normal(ks[2], (DEC_BATCH, DEC_SEQ, D_MODEL), F32),
        "state_hgrn": 0.5 * jax.random.normal(ks[3], (DEPTH, DEC_BATCH, HGRN_HEADS, HGRN_HEAD_DIM, HGRN_HEAD_DIM), F32),
        "state_ret": 0.5 * jax.random.normal(ks[4], (DEPTH, DEC_BATCH, RET_HEADS, RET_HEAD_DIM, RET_HEAD_DIM), F32),
        "w_in": w_in,
        "lb_logits": 0.5 * jax.random.normal(ks[5], (DEPTH + 1, HGRN_WIDTH), F32),
        "hgrn_norm_g": 1.0 + 0.02 * jax.random.normal(ks[6], (DEPTH, HGRN_HEADS, HGRN_HEAD_DIM), F32),
        "ret_norm_g": 1.0 + 0.02 * jax.random.normal(ks[7], (DEPTH, RET_HEADS, RET_HEAD_DIM), F32),
        "w_out": jax.random.normal(ks[8], (DEPTH, MIX_WIDTH, D_MODEL), F32) * (MIX_WIDTH ** -0.5) * DEEPNORM_BETA,
        "ln1_g": 1.0 + 0.02 * jax.random.normal(ks[9], (DEPTH, D_MODEL), F32),
        "ln1_b": 0.02 * jax.random.normal(ks[10], (DEPTH, D_MODEL), F32),
        "w_gate": jax.random.normal(ks[11], (DEPTH, D_MODEL, D_FF), F32) * (D_MODEL ** -0.5) * DEEPNORM_BETA,
        "w_up": jax.random.normal(ks[12], (DEPTH, D_MODEL, D_FF), F32) * (D_MODEL ** -0.5) * DEEPNORM_BETA,
        "w_down": jax.random.normal(ks[13], (DEPTH, D_FF, D_MODEL), F32) * (D_FF ** -0.5) * DEEPNORM_BETA,
        "ln2_g": 1.0 + 0.02 * jax.random.normal(ks[14], (DEPTH, D_MODEL), F32),
        "ln2_b": 0.02 * jax.random.normal(ks[15], (DEPTH, D_MODEL), F32),
    }


def reference(x_prompt, x_sample, state_hgrn, state_ret, w_in, lb_logits, hgrn_norm_g, ret_norm_g,
              w_out, ln1_g, ln1_b, w_gate, w_up, w_down, ln2_g, ln2_b):
    B, T = x_prompt.shape[0], x_prompt.shape[1]
    Bd, Td = x_sample.shape[0], x_sample.shape[1]
    pos_prompt = jnp.arange(T, dtype=F32)
    pos_sample = jnp.arange(Td, dtype=F32) + float(PAST_LEN)
    zero_h = jnp.zeros((DEPTH, B, HGRN_HEADS, HGRN_HEAD_DIM, HGRN_HEAD_DIM), x_prompt.dtype)
    zero_r = jnp.zeros((DEPTH, B, RET_HEADS, RET_HEAD_DIM, RET_HEAD_DIM), x_prompt.dtype)
    y_prompt, hgrn_state_prompt, ret_state_prompt = _trunk(
        x_prompt, pos_prompt, zero_h, zero_r, w_in, lb_logits, hgrn_norm_g, ret_norm_g, w_out,
        ln1_g, ln1_b, w_gate, w_up, w_down, ln2_g, ln2_b)
    y_sample, hgrn_state_sample, ret_state_sample = _trunk(
        x_sample, pos_sample, state_hgrn, state_ret, w_in, lb_logits, hgrn_norm_g, ret_norm_g, w_out,
        ln1_g, ln1_b, w_gate, w_up, w_down, ln2_g, ln2_b)
    return (y_prompt, y_sample, hgrn_state_prompt, ret_state_prompt, hgrn_state_sample, ret_state_sample)
```

```python
import contextlib
import math
import numpy as np
import concourse.bass as bass
import concourse.mybir as mybir
from concourse.bass_utils import run_bass_kernel_spmd

F32 = mybir.dt.float32
BF16 = mybir.dt.bfloat16
ALU = mybir.AluOpType
AF = mybir.ActivationFunctionType

D = 1024
T = 2048
NSQ = 16
NS = 64
NT = T + 128
DFF = 2816
NFC = DFF // 128
ALPHA = 2.0 ** 0.25
EPS = 1e-5
SCALE = 128.0 ** -0.5
BLOCKS = [(0, 512, "p"), (512, 512, "p"), (1024, 512, "p"), (1536, 512, "p"), (2048, 128, "s")]
FGROUPS = [[0, 1, 2, 3], [4, 5, 6, 7], [8, 9, 10, 11], [12, 13, 14, 15], [16, 17, 18], [19, 20, 21]]
ENGINES = ("tensor", "vector", "scalar", "gpsimd", "sync")


class Sched:
    def __init__(self, nc, n_dma_sems=4):
        self.nc = nc
        self.ops = []
        self.last_writer = {}
        self.readers = {}
        self.n_dma_sems = n_dma_sems
        self.pending_barrier = {}

    stopped = False

    def op(self, eng, fn, reads=(), writes=(), dma=False):
        if self.stopped:
            return None
        idx = len(self.ops)
        deps = set()
        for r in reads:
            w = self.last_writer.get(r)
            if w is not None:
                deps.add(w)
        for w in writes:
            lw = self.last_writer.get(w)
            if lw is not None:
                deps.add(lw)
            for rd in self.readers.get(w, ()):
                deps.add(rd)
        pb = self.pending_barrier.pop(eng, None)
        if pb:
            deps |= pb
        deps.discard(idx)
        self.ops.append(dict(eng=eng, fn=fn, deps=deps, dma=dma, signal=False))
        for r in reads:
            self.readers.setdefault(r, []).append(idx)
        for w in writes:
            self.last_writer[w] = idx
            self.readers[w] = []
        return idx

    def dma(self, fn, reads=(), writes=(), queue="sync"):
        return self.op(queue, fn, reads, writes, dma=True)

    def barrier(self):
        deps = set()
        last = {}
        for i, o in enumerate(self.ops):
            if o["dma"]:
                deps.add(i)
            else:
                last[o["eng"]] = i
        deps |= set(last.values())
        for e in ENGINES:
            self.pending_barrier[e] = set(deps) | self.pending_barrier.get(e, set())

    def emit(self, final_wait_engine="sync"):
        nc = self.nc
        ops = self.ops
        for i, o in enumerate(ops):
            for d in o["deps"]:
                p = ops[d]
                if p["dma"] or p["eng"] != o["eng"] or o["eng"] != "tensor":
                    p["signal"] = True
        for o in ops:
            if o["dma"]:
                o["signal"] = True
        cnt = {e: 0 for e in ENGINES}
        dma_cnt = {e: 0 for e in ENGINES}
        dma_val = {}
        for i, o in enumerate(ops):
            if not o["signal"]:
                continue
            if o["dma"]:
                q = o["eng"]
                k = dma_cnt[q] % self.n_dma_sems
                dma_cnt[q] += 1
                key = (q, k)
                prev = dma_val.get(key, 0)
                o["prev_tick"] = (key, prev) if prev else None
                dma_val[key] = prev + 16
                o["tick"] = (key, prev + 16)
            else:
                cnt[o["eng"]] += 1
                o["tick"] = (o["eng"], cnt[o["eng"]])
        with contextlib.ExitStack() as st:
            sems = {}
            for e in ENGINES:
                sems[e] = st.enter_context(nc.semaphore("s_" + e))
            for key in dma_val:
                sems[key] = st.enter_context(nc.semaphore("d_%s_%d" % key))
            block = st.enter_context(nc.Block())
            for e in ENGINES:
                my = [(i, o) for i, o in enumerate(ops) if o["eng"] == e]
                if not my and e != final_wait_engine:
                    continue

                def body(eng, e=e, my=my):
                    seen = {}
                    for i, o in my:
                        need = {}
                        for d in o["deps"]:
                            p = ops[d]
                            if not p["signal"]:
                                continue
                            if (not p["dma"]) and p["eng"] == e and e == "tensor":
                                continue
                            k, v = p["tick"]
                            if v > need.get(k, 0):
                                need[k] = v
                        if o["dma"] and o.get("prev_tick"):
                            k, v = o["prev_tick"]
                            if v > need.get(k, 0):
                                need[k] = v
                        for k, v in need.items():
                            if seen.get(k, 0) >= v:
                                continue
                            eng.wait_ge(sems[k], v)
                            seen[k] = v
                        ins = o["fn"](eng)
                        if o["signal"]:
                            if o["dma"]:
                                ins.then_inc(sems[o["tick"][0]], 16)
                            else:
                                ins.then_inc(sems[e], 1)
                    if e == final_wait_engine:
                        for key, v in dma_val.items():
                            eng.wait_ge(sems[key], v)

                getattr(block, e)(body)
        return cnt


def _consts():
    c = {}
    p = np.arange(128)
    gam = 1.0 - 2.0 ** (-5.0 - np.arange(4, dtype=np.float64))
    j = p[:, None]
    i = p[None, :]
    c["maskH"] = ((j // 64 == i // 64) & (i >= j)).astype(np.float64)
    j6 = np.arange(64)[:, None]
    i6 = np.arange(64)[None, :]
    mS = ((j6 // 4 == i6 // 4) & (i6 >= j6))
    mh = np.zeros((128, 128))
    mh[:64, :64] = mS
    c["maskHS"] = mh
    for h in range(4):
        m = np.where(i >= j, SCALE * gam[h] ** (-(j + 1.0)), 0.0)
        c["maskR%d" % h] = m
        ms = np.zeros((128, 128))
        ms[:64, :64] = np.where(mS, SCALE * gam[h] ** (-((j6 % 4) + 1.0)), 0.0)
        c["maskRS%d" % h] = ms
    ksc = np.zeros((128, 8))
    for h in range(4):
        ksc[:, h] = SCALE * gam[h] ** (127.0 - p)
        ksc[:64, 4 + h] = SCALE * gam[h] ** (3.0 - (np.arange(64) % 4))
    c["ksc"] = ksc
    sm = np.zeros((128, 16))
    sm[:64] = (np.arange(64)[:, None] // 4 == np.arange(16)[None, :])
    c["seqmask"] = sm
    scm = np.ones((128, 512))
    scm[:, ::64] = 0.0
    c["scanm"] = scm
    scs = np.ones((128, 128))
    scs[:, ::4] = 0.0
    c["scanmS"] = scs
    c["ident"] = np.eye(128)
    c["ones"] = np.ones((128, 128))
    c["eps"] = np.full((128, 1), EPS)
    c["one"] = np.ones((128, 1))
    c["rm0"] = (p < 64).astype(np.float64)[:, None]
    c["rm1"] = (p >= 64).astype(np.float64)[:, None]
    offs = {}
    o = 0
    arrs = []
    for k, v in c.items():
        offs[k] = (o, v.shape[1])
        o += v.shape[1]
        arrs.append(v)
    packed = np.concatenate(arrs, axis=1).astype(np.float32)
    pos = np.concatenate([np.arange(T, dtype=np.float64), 16384.0 + (np.arange(128) % 4)])
    tl = np.concatenate([np.arange(T) % 128, np.arange(128) % 4]).astype(np.float64)
    inv = 10000.0 ** (-(np.arange(64, dtype=np.float64)) / 64.0)
    ang = inv[p % 64][:, None] * pos[None, :]
    sgn = np.where(p < 64, -1.0, 1.0)[:, None]
    cos = np.cos(ang)
    sin = np.sin(ang) * sgn
    ropek = np.stack([cos, sin]).astype(np.float32)
    ropeq = np.zeros((4, 2, 128, NT), np.float32)
    for h in range(4):
        g = gam[h] ** (tl + 1.0)
        ropeq[h, 0] = cos * g[None, :]
        ropeq[h, 1] = sin * g[None, :]
    decays = [(float(gam[h] ** 128.0), float(gam[h] ** 4.0)) for h in range(4)]
    return packed, offs, ropeq, ropek, decays


_CONSTS = _consts()


def build_nc(stop_at=None):
    packed, offs, _, _, decays = _CONSTS
    NCOL = packed.shape[1]
    nc = bass.Bass("TRN2", target_bir_lowering=False)

    def din(name, shape):
        return nc.dram_tensor(name, list(shape), F32, kind="ExternalInput").ap()

    def dout(name, shape):
        return nc.dram_tensor(name, list(shape), F32, kind="ExternalOutput").ap()

    xp = din("xp", [T, D])
    xs = din("xs", [NS, D])
    sth = din("sth", [NSQ, 4, 128, 128])
    strr = din("str", [NSQ, 4, 128, 128])
    w_in = din("w_in", [D, 4096])
    w_out = din("w_out", [D, D])
    w_gate = din("w_gate", [D, DFF])
    w_up = din("w_up", [D, DFF])
    w_down = din("w_down", [DFF, D])
    lbl = din("lbl", [128, 8])
    gcol_d = din("gcol", [128, 8])
    lnb = din("lnb", [4, 128, D])
    cst_d = din("cst", [128, NCOL])
    ropeq_d = din("ropeq", [4, 2, 128, NT])
    ropek_d = din("ropek", [2, 128, NT])

    yp = dout("yp", [T, D])
    ys = dout("ys", [NS, D])
    hsp = dout("hsp", [4, 128, 128])
    rsp = dout("rsp", [4, 128, 128])
    hss = dout("hss", [NSQ, 4, 128, 128])
    rss = dout("rss", [NSQ, 4, 128, 128])

    w_in_v = w_in.rearrange("(k p) c -> p k c", p=128)
    w_gate_v = w_gate.rearrange("(k p) c -> p k c", p=128)
    w_up_v = w_up.rearrange("(k p) c -> p k c", p=128)

    S = Sched(nc)

    def chk(tag):
        if stop_at is not None and tag == stop_at:
            S.stopped = True

    with contextlib.ExitStack() as g:
        def sb(name, shape, dt=F32, st=g):
            return st.enter_context(nc.sbuf_tensor(name, list(shape), dt))

        B = [g.enter_context(nc.psum_tensor("B%d" % i, [128, 512], F32)) for i in range(8)]
        Bbf = [b.bitcast(BF16) for b in B]
        inT = sb("inT", [128, 8, NT], BF16)
        cst = sb("cst_sb", [128, NCOL])
        ident = sb("ident_bf", [128, 128], BF16)
        ones = sb("ones_bf", [128, 128], BF16)
        NST = 3
        wst = [sb("wst%d" % i, [128, 1024]) for i in range(NST)]
        wst_ctr = [0]

        def C(name, rows=128):
            o, w = offs[name]
            return cst[0:rows, o:o + w]

        S.dma(lambda e: e.dma_start(out=cst[:], in_=cst_d), writes=["cst"])
        S.op("gpsimd", lambda e: e.tensor_copy(out=ident[:], in_=C("ident")), reads=["cst"], writes=["ident"])
        S.op("gpsimd", lambda e: e.tensor_copy(out=ones[:], in_=C("ones")), reads=["cst"], writes=["ones"])

        pending_casts = []

        def flush_casts():
            while pending_casts:
                pending_casts.pop(0)()

        def load_cast(src, dst, dst_key, shape3=None, scale=None, scale_key=None, defer=False):
            slot = wst_ctr[0] % NST
            wst_ctr[0] += 1
            stg = wst[slot]
            if shape3 is not None:
                sv = stg[:].rearrange("p (a b) -> p a b", a=shape3[0])
            else:
                sv = stg[:]
            S.dma(lambda e: e.dma_start(out=sv, in_=src), writes=[("wst", slot)])
            if defer:
                pending_casts.append(lambda: S.op("vector", lambda e: e.tensor_copy(out=dst, in_=sv), reads=[("wst", slot)], writes=[dst_key]))
                return
            if scale is None:
                S.op("gpsimd", lambda e: e.tensor_copy(out=dst, in_=sv), reads=[("wst", slot)], writes=[dst_key])
            else:
                S.op("gpsimd", lambda e: e.tensor_scalar(out=dst, in0=sv, scalar1=scale, scalar2=None, op0=ALU.mult),
                     reads=[("wst", slot), scale_key], writes=[dst_key])

        with contextlib.ExitStack() as p1:
            def sb1(name, shape, dt=F32):
                return sb(name, shape, dt, st=p1)

            mixT = sb("mixT", [128, 8, NT], BF16)
            wout = sb("wout", [128, 8, D], BF16)
            gcol = sb("gcol_sb", [128, 8])
            S.dma(lambda e: e.dma_start(out=gcol[:], in_=gcol_d), writes=["gcol"])

            def load_wout(which=range(8)):
                for hh_ in which:
                    slot = wst_ctr[0] % NST
                    wst_ctr[0] += 1
                    S.dma(lambda e, slot=slot, hh_=hh_: e.dma_start(out=wst[slot][:], in_=w_out[hh_ * 128:(hh_ + 1) * 128, :]), writes=[("wst", slot)])
                    pending_casts.append(lambda slot=slot, hh_=hh_: S.op(
                        "vector", lambda e: e.tensor_scalar(out=wout[:, hh_, :], in0=wst[slot][:], scalar1=gcol[:, hh_:hh_ + 1], scalar2=None, op0=ALU.mult),
                        reads=[("wst", slot), "gcol"], writes=[("wout", hh_)]))
            wh = [sb1("wh%d" % i, [128, 4, 8, 128], BF16) for i in range(2)]
            lbt = sb1("lbt", [128, 8])
            lbd = sb1("lbd", [128, 4])
            lb = sb1("lb", [128, 4])
            oml = sb1("oml", [128, 4])
            noml = sb1("noml", [128, 4])
            tf = {n: sb1("t_" + n, [128, 512]) for n in ["a0", "a1", "a2", "a3", "a4", "a5", "a6", "std", "rstd", "on"]}
            for a, (n1, n2) in enumerate([("q", "qs"), ("sig", "ks"), ("logf", "t1"), ("k", "t2"), ("b", "t3"), ("enb", "t4"), ("kt", "kt_")]):
                tf[n1] = tf["a%d" % a]
                tf[n2] = tf["a%d" % a]
            TK = {"q": "a0", "qs": "a0", "sig": "a1", "ks": "a1", "logf": "a2", "t1": "a2", "k": "a3", "t2": "a3", "b": "a4", "t3": "a4",
                  "enb": "a5", "t4": "a5", "kt": "a6"}
            tabs = [sb1("tab%d" % i, [128, 512]) for i in range(4)]
            gate = [sb1("gate%d" % i, [128, 512]) for i in range(3)]
            ebb = [sb1("eb%d" % i, [128, 512]) for i in range(3)]
            qt = [sb1("qt%d" % i, [128, 512], BF16) for i in range(3)]
            ktb = [sb1("ktb%d" % i, [128, 512], BF16) for i in range(3)]
            kh = [sb1("kh%d" % i, [128, 512], BF16) for i in range(3)]
            vbf = [sb1("vbf%d" % i, [128, 512], BF16) for i in range(3)]
            sq = sb1("sq", [128, 512], BF16)
            khtok = sb1("khtok", [128, 4, 128], BF16)
            khtok2 = sb1("khtok2", [128, 4, 128], BF16)
            atm = sb1("atm", [128, 4, 128], BF16)
            khm = sb1("khm", [128, 16, 128], BF16)
            NSB = 16
            Sbf = sb1("Sbf", [128, NSB, 128], BF16)
            S32 = sb1("S32", [128, 2, 128])
            st32 = [sb1("st32_0", [128, 16, 128])]
            stbf = sb1("stbf", [128, 16, 128], BF16)
            xst = [st32[0][:, 8 * i:8 * (i + 1), :].rearrange("p a b -> p (a b)") for i in range(2)]
            xbf = [stbf[:, 8 * i:8 * (i + 1), :].rearrange("p a b -> p (a b)") for i in range(2)]

            S.dma(lambda e: e.dma_start(out=lbt[:], in_=lbl), writes=["lbt"])
            S.op("vector", lambda e: e.tensor_tensor(out=lbd[:], in0=lbt[:, 0:4], in1=lbt[:, 4:8], op=ALU.subtract), reads=["lbt"], writes=["lbd"])
            S.op("scalar", lambda e: e.activation(out=lb[:], in_=lbd[:], func=AF.Sigmoid), reads=["lbd"], writes=["lb"])
            S.op("vector", lambda e: e.tensor_scalar(out=oml[:], in0=lb[:], scalar1=-1.0, scalar2=1.0, op0=ALU.mult, op1=ALU.add), reads=["lb"], writes=["oml"])
            S.op("vector", lambda e: e.tensor_scalar(out=noml[:], in0=lb[:], scalar1=-1.0, scalar2=None, op0=ALU.add), reads=["lb"], writes=["noml"])

            def transpose_tile(src_bf, rows, tcol, pbank, ci, dst, dst_key, src_key):
                pv = Bbf[pbank][:, 0:1024].rearrange("p (k c) -> p k c", k=8)
                for k in range(8):
                    S.op("tensor", lambda e, k=k: e.transpose(out=pv[:, k, :], in_=src_bf[:, k * 128:(k + 1) * 128], identity=ident[:, :]),
                         reads=[src_key, "ident"], writes=[("B", pbank)])
                if ci % 2 == 0:
                    S.op("scalar", lambda e: e.activation(out=dst[:, :, tcol:tcol + 128], in_=pv[:, :, :], func=AF.Copy),
                         reads=[("B", pbank)], writes=[dst_key])
                else:
                    S.op("vector", lambda e: e.tensor_copy(out=dst[:, :, tcol:tcol + 128], in_=pv[:, :, :]),
                         reads=[("B", pbank)], writes=[dst_key])

            for t in range(17):
                rows = 128 if t < 16 else 64
                src = xp[t * 128:(t + 1) * 128, :] if t < 16 else xs
                sl = t % 2
                if rows < 128:
                    S.op("gpsimd", lambda e, sl=sl, rows=rows: e.memset(xst[sl][rows:128, :], 0.0), writes=[("xst", sl)])
                S.dma(lambda e, sl=sl, rows=rows, src=src: e.dma_start(out=xst[sl][0:rows, :], in_=src), writes=[("xst", sl)])
                if t % 2 == 0:
                    S.op("vector", lambda e, sl=sl: e.tensor_copy(out=xbf[sl][:, :], in_=xst[sl][:, :]), reads=[("xst", sl)], writes=[("xbf", sl)])
                else:
                    S.op("gpsimd", lambda e, sl=sl: e.tensor_copy(out=xbf[sl][:, :], in_=xst[sl][:, :]), reads=[("xst", sl)], writes=[("xbf", sl)])
                transpose_tile(xbf[sl], rows, t * 128, 4 + sl, t, inT, ("inT", min(t // 4, 4)), ("xbf", sl))
            S.barrier()

            chk("p0")
            def load_head_weights(h, secs=(0, 1, 2, 3)):
                slot = h % 2
                base = 0 if h < 4 else 2048
                hh = h % 4
                for sec in secs:
                    c0 = base + sec * 512 + hh * 128
                    load_cast(w_in_v[:, :, c0:c0 + 128], wh[slot][:, sec, :, :], ("wh", slot, sec), shape3=(8, 128), defer=(len(secs) == 1))

            def proj_fm(h, bi, sec, bank):
                c0, N, _ = BLOCKS[bi]
                slot = h % 2
                for k in range(8):
                    S.op("tensor", lambda e, k=k: e.matmul(B[bank][:, 0:N], lhsT=wh[slot][:, sec, k, :], rhs=inT[:, k, c0:c0 + N], start=(k == 0), stop=(k == 7)),
                         reads=[("wh", slot, sec), ("inT", bi)], writes=[("B", bank)])
                    if k % 4 == 3:
                        yield

            def proj_tm(h, bi, sec, bank):
                c0, N, kind = BLOCKS[bi]
                slot = h % 2
                rows = 128
                for t in range(max(N // 128, 1)):
                    for k in range(8):
                        S.op("tensor", lambda e, k=k, t=t: e.matmul(B[bank][0:rows, t * 128:(t + 1) * 128], lhsT=inT[:, k, c0 + t * 128:c0 + t * 128 + rows],
                                                                     rhs=wh[slot][:, sec, k, :], start=(k == 0), stop=(k == 7)),
                             reads=[("wh", slot, sec), ("inT", bi)], writes=[("B", bank)])
                    yield

            def gen_proj(h, bi):
                flush_casts()
                if bi < 4 and h + 1 < 8:
                    load_head_weights(h + 1, secs=(bi,))
                if h == 3 and bi < 4:
                    load_wout(which=(2 * bi, 2 * bi + 1))
                yield from proj_fm(h, bi, 3, 2)
                yield from proj_fm(h, bi, 0, 0)
                yield from proj_tm(h, bi, 2, 3)
                yield from proj_fm(h, bi, 1, 1)

            def act(out, in_, func, reads, writes, **kw):
                S.op("scalar", lambda e: e.activation(out=out, in_=in_, func=func, **kw), reads=reads, writes=writes)

            def tt(eng, out, in0, in1, op, reads, writes):
                S.op(eng, lambda e: e.tensor_tensor(out=out, in0=in0, in1=in1, op=op), reads=reads, writes=writes)

            def gen_elem(h, bi, par, gi):
                c0, N, kind = BLOCKS[bi]
                is_h = h < 4
                hh = h % 4
                smp = kind == "s"
                V = lambda n: tf[n][:, 0:N]
                K_ = lambda n: TK[n]
                G = gate[gi][:, 0:N]
                EB = ebb[par][:, 0:N]
                act(G, B[2][:, 0:N], AF.Silu, [("B", 2)], [("gate", gi)])
                if is_h:
                    act(V("q"), B[0][:, 0:N], AF.Silu, [("B", 0)], [K_("q")])
                    act(vbf[par][:, 0:N], B[3][:, 0:N], AF.Copy, [("B", 3)], [("vbf", par)])
                    act(V("sig"), B[1][:, 0:N], AF.Sigmoid, [("B", 1)], [K_("sig")])
                else:
                    for ti, src in enumerate([ropeq_d[hh, 0], ropeq_d[hh, 1], ropek_d[0], ropek_d[1]]):
                        S.dma(lambda e, ti=ti, src=src: e.dma_start(out=tabs[ti][:, 0:N], in_=src[:, c0:c0 + N]), writes=[("tab", ti)])
                    for nm, bank, tc, ta in (("qs", 0, 0, "t1"), ("ks", 1, 2, "t3")):
                        act(tf[nm][0:64, 0:N], B[bank][64:128, 0:N], AF.Copy, [("B", bank)], [K_(nm) + "lo"])
                        act(tf[nm][64:128, 0:N], B[bank][0:64, 0:N], AF.Copy, [("B", bank)], [K_(nm) + "hi"])
                        tt("vector", V(ta), B[bank][:, 0:N], tabs[tc][:, 0:N], ALU.mult, [("B", bank), ("tab", tc)], [K_(ta)])
                        if bank == 0:
                            act(vbf[par][:, 0:N], B[3][:, 0:N], AF.Copy, [("B", 3)], [("vbf", par)])
                yield "EVAC_DONE"
                if is_h:
                    act(V("logf"), V("sig"), AF.Ln, [K_("sig"), "oml", "lb"], [K_("logf")], scale=oml[:, hh:hh + 1], bias=lb[:, hh:hh + 1])
                    yield
                    S.op("vector", lambda e: e.tensor_scalar(out=V("k"), in0=V("sig"), scalar1=noml[:, hh:hh + 1], scalar2=oml[:, hh:hh + 1], op0=ALU.mult, op1=ALU.add),
                         reads=[K_("sig"), "noml", "oml"], writes=[K_("k")])
                    yield
                    scm = C("scanmS")[:, 0:N] if smp else C("scanm")[:, 0:N]
                    S.op("vector", lambda e: e.tensor_tensor_scan(out=V("b"), data0=scm, data1=V("logf"), initial=0.0, op0=ALU.mult, op1=ALU.add),
                         reads=[K_("logf"), "cst"], writes=[K_("b")])
                    yield
                    act(EB, V("b"), AF.Exp, [K_("b")], [("eb", par)])
                    yield
                    act(V("enb"), V("b"), AF.Exp, [K_("b")], [K_("enb")], scale=-1.0)
                    yield
                    tt("vector", qt[par][:, 0:N], V("q"), EB, ALU.mult, [K_("q"), ("eb", par)], [("qt", par)])
                    yield
                    tt("vector", V("kt"), V("k"), V("enb"), ALU.mult, [K_("k"), K_("enb")], [K_("kt")])
                    yield
                    act(ktb[par][:, 0:N], V("kt"), AF.Copy, [K_("kt")], [("ktb", par)])
                    yield
                    CL = 4 if smp else 64
                    ebl = EB.rearrange("p (c t) -> p c t", t=CL)[:, :, CL - 1:CL].to_broadcast([128, N // CL, CL])
                    S.op("gpsimd", lambda e: e.tensor_tensor(out=kh[par][:, 0:N].rearrange("p (c t) -> p c t", t=CL), in0=V("kt").rearrange("p (c t) -> p c t", t=CL), in1=ebl, op=ALU.mult),
                         reads=[K_("kt"), ("eb", par)], writes=[("kh", par)])
                    yield
                else:
                    for nm, tsn, ta, tb_, dst, dkey in (("qs", 1, "t1", "t2", qt[par], ("qt", par)), ("ks", 3, "t3", "t4", ktb[par], ("ktb", par))):
                        tt("gpsimd", V(tb_), V(nm), tabs[tsn][:, 0:N], ALU.mult, [K_(nm) + "lo", K_(nm) + "hi", ("tab", tsn)], [K_(tb_)])
                        yield
                        tt("vector", dst[:, 0:N], V(ta), V(tb_), ALU.add, [K_(ta), K_(tb_)], [dkey])
                        yield

            def bctx(h, bi, par):
                c0, N, kind = BLOCKS[bi]
                return dict(c0=c0, N=N, is_h=h < 4, hh=h % 4, smp=kind == "s", ntile=N // 128,
                            QT=qt[par], KTB=ktb[par], VBF=vbf[par], QK=("qt", par), KK=("ktb", par), VK=("vbf", par), EBF=ebb[par])

            def gen_pre(h, bi, par):
                x = bctx(h, bi, par)
                is_h, hh, smp, ntile = x["is_h"], x["hh"], x["smp"], x["ntile"]
                QT, KTB, VBF, QK, KK, VK = x["QT"], x["KTB"], x["VBF"], x["QK"], x["KK"], x["VK"]
                ksrc, ksrc_key = (kh[par], ("kh", par)) if is_h else (ktb[par], ("ktb", par))
                A4 = B[4][:, :].rearrange("p (t c) -> p t c", t=4)
                TR = Bbf[5][:, 0:512].rearrange("p (t c) -> p t c", t=4)
                for t in range(ntile):
                    cs = slice(t * 128, t * 128 + 128)
                    S.op("tensor", lambda e, t=t, cs=cs: e.matmul(A4[:, t, :], lhsT=KTB[:, cs], rhs=QT[:, cs], start=True, stop=True),
                         reads=[KK, QK], writes=[("B", 4)])
                for t in range(ntile):
                    cs = slice(t * 128, t * 128 + 128)
                    S.op("tensor", lambda e, t=t, cs=cs: e.transpose(out=TR[:, t, :], in_=ksrc[:, cs], identity=ident[:, :]),
                         reads=[ksrc_key, "ident"], writes=[("B", 5)])
                yield
                if is_h:
                    mk = C("maskHS") if smp else C("maskH")
                else:
                    mk = C("maskRS%d" % hh) if smp else C("maskR%d" % hh)
                mkb = mk.unsqueeze(1).to_broadcast([128, ntile, 128])
                tt("vector", atm[:, 0:ntile, :], A4[:, 0:ntile, :], mkb, ALU.mult, [("B", 4), "cst"], ["atm"])
                if is_h and not smp:
                    act(khtok[:, 0:ntile, :], TR[:, 0:ntile, :], AF.Copy, [("B", 5), "cst"], ["khtok"], scale=C("rm0"))
                    act(khtok2[:, 0:ntile, :], TR[:, 0:ntile, :], AF.Copy, [("B", 5), "cst"], ["khtok2"], scale=C("rm1"))
                elif is_h:
                    act(khtok[:, 0:ntile, :], TR[:, 0:ntile, :], AF.Copy, [("B", 5)], ["khtok"])
                else:
                    kc = 4 + hh if smp else hh
                    act(khtok[:, 0:ntile, :], TR[:, 0:ntile, :], AF.Copy, [("B", 5), "cst"], ["khtok"], scale=C("ksc")[:, kc:kc + 1])
                yield
                if not smp:
                    nch = 2 if is_h else 1
                    for t in range(ntile):
                        for c in range(nch):
                            bank = 6 + c
                            ksb = khtok2 if c == 1 else khtok
                            S.op("tensor", lambda e, t=t, bank=bank, ksb=ksb: e.matmul(B[bank][:, t * 128:(t + 1) * 128], lhsT=ksb[:, t, :],
                                                                                     rhs=VBF[:, t * 128:(t + 1) * 128], start=True, stop=True),
                                 reads=["khtok", "khtok2", VK], writes=[("B", bank)])
                    yield

            def gen_chain(h, bi, par, st):
                x = bctx(h, bi, par)
                is_h, hh, smp, ntile, EBF = x["is_h"], x["hh"], x["smp"], x["ntile"], x["EBF"]
                if bi == 0:
                    st["gc"] = 0
                    st["sp"] = 0
                    S.op("gpsimd", lambda e: e.memset(Sbf[:, 0, :], 0.0), writes=[("Sbf", 0)])
                    S.op("gpsimd", lambda e: e.memset(S32[:, 0, :], 0.0), writes=[("S32", 0)])
                    yield
                st["gc0"] = st["gc"]
                if smp:
                    return
                nch = 2 if is_h else 1
                for t in range(ntile):
                    for c in range(nch):
                        bank = 6 + c
                        pc = B[bank][:, t * 128:(t + 1) * 128]
                        sp = st["sp"]
                        if is_h:
                            col = t * 128 + c * 64 + 63
                            S.op("vector", lambda e, pc=pc, col=col, sp=sp: e.scalar_tensor_tensor(out=S32[:, 1 - sp, :], in0=S32[:, sp, :], scalar=EBF[:, col:col + 1], in1=pc, op0=ALU.mult, op1=ALU.add),
                                 reads=[("S32", sp), ("eb", par), ("B", bank)], writes=[("S32", 1 - sp)])
                        else:
                            S.op("vector", lambda e, pc=pc, sp=sp: e.scalar_tensor_tensor(out=S32[:, 1 - sp, :], in0=S32[:, sp, :], scalar=decays[hh][0], in1=pc, op0=ALU.mult, op1=ALU.add),
                                 reads=[("S32", sp), ("B", bank)], writes=[("S32", 1 - sp)])
                        st["sp"] = 1 - sp
                        st["gc"] += 1
                        nslot = st["gc"] % NSB
                        yield
                        act(Sbf[:, nslot, :], S32[:, 1 - sp, :], AF.Copy, [("S32", 1 - sp)], [("Sbf", nslot)])
                        yield

            def gen_tail(h, bi, par, gi, st):
                x = bctx(h, bi, par)
                c0, N, is_h, hh, smp, ntile = x["c0"], x["N"], x["is_h"], x["hh"], x["smp"], x["ntile"]
                QT, KTB, VBF, QK, KK, VK, EBF = x["QT"], x["KTB"], x["VBF"], x["QK"], x["KK"], x["VK"], x["EBF"]
                V = lambda n: tf[n][:, 0:N]
                if not smp:
                    nch = 2 if is_h else 1
                    kk = 128 // nch
                    gc = st["gc0"]
                    for t in range(ntile):
                        tcs = slice(t * 128, (t + 1) * 128)
                        S.op("tensor", lambda e, t=t, tcs=tcs: e.matmul(B[4][:, tcs], lhsT=VBF[:, tcs], rhs=atm[:, t, :], start=True, stop=False),
                             reads=[VK, "atm"], writes=[("B", 4)])
                        for c in range(nch):
                            slot = gc % NSB
                            gc += 1
                            ccs = slice(t * 128 + c * kk, t * 128 + (c + 1) * kk)
                            S.op("tensor", lambda e, slot=slot, ccs=ccs, c=c: e.matmul(B[4][:, ccs], lhsT=Sbf[:, slot, :], rhs=QT[:, ccs], start=False, stop=(c == nch - 1)),
                                 reads=[("Sbf", slot), QK], writes=[("B", 4)])
                    if bi == 3:
                        dst = (hsp if h < 4 else rsp)[h % 4]
                        sp = st["sp"]
                        S.dma(lambda e, dst=dst, sp=sp: e.dma_start(out=dst, in_=S32[:, sp, :]), reads=[("S32", sp)])
                else:
                    s32 = st32[0]
                    s32k = ("st32", 0)
                    S.op("gpsimd", lambda e: e.tensor_copy(out=stbf[:], in_=s32[:]), reads=[s32k], writes=["stbf"])
                    S.op("tensor", lambda e: e.matmul(B[4][:, 0:128], lhsT=VBF[:, 0:128], rhs=atm[:, 0, :], start=True, stop=False),
                         reads=[VK, "atm"], writes=[("B", 4)])
                    for s_ in range(NSQ):
                        S.op("tensor", lambda e, s_=s_: e.matmul(B[4][:, 4 * s_:4 * s_ + 4], lhsT=stbf[:, s_, :], rhs=QT[:, 4 * s_:4 * s_ + 4], start=False, stop=(s_ == NSQ - 1)),
                             reads=["stbf", QK], writes=[("B", 4)])
                act(sq[:, 0:N], B[4][:, 0:N], AF.Square, [("B", 4)], ["sq"])
                S.op("tensor", lambda e: e.matmul(B[5][:, 0:N], lhsT=ones[:, :], rhs=sq[:, 0:N], start=True, stop=True), reads=["ones", "sq"], writes=[("B", 5)])
                act(V("std"), B[5][:, 0:N], AF.Ln, [("B", 5), "cst"], ["std"], scale=1.0 / 128.0, bias=C("eps"))
                act(V("rstd"), V("std"), AF.Exp, ["std"], ["rstd"], scale=-0.5)
                tt("vector", V("on"), B[4][:, 0:N], V("rstd"), ALU.mult, [("B", 4), "rstd"], ["on"])
                tt("gpsimd", mixT[:, h, c0:c0 + N], V("on"), gate[gi][:, 0:N], ALU.mult, ["on", ("gate", gi)], [("mixT", bi)])
                if smp:
                    s32 = st32[0]
                    s32k = ("st32", 0)
                    S.op("gpsimd", lambda e: e.tensor_tensor(out=khm[:, :, :], in0=khtok[:, 0:1, :].to_broadcast([128, 16, 128]),
                                                             in1=C("seqmask").unsqueeze(2).to_broadcast([128, 16, 128]), op=ALU.mult),
                         reads=["khtok", "cst"], writes=["khm"])
                    for sg_ in range(4):
                        bank = 6 + (sg_ % 2)
                        for si in range(4):
                            s_ = sg_ * 4 + si
                            S.op("tensor", lambda e, s_=s_, si=si, bank=bank: e.matmul(B[bank][:, si * 128:(si + 1) * 128], lhsT=khm[:, s_, :], rhs=VBF[:, 0:128], start=True, stop=True),
                                 reads=["khm", VK], writes=[("B", bank)])
                        sv = s32[:, sg_ * 4:(sg_ + 1) * 4, :]
                        if is_h:
                            ebs = EBF[:, 0:64].rearrange("p (s t) -> p s t", t=4)[:, sg_ * 4:(sg_ + 1) * 4, 3:4].to_broadcast([128, 4, 128])
                            S.op("vector", lambda e, sv=sv, ebs=ebs: e.tensor_tensor(out=sv, in0=sv, in1=ebs, op=ALU.mult), reads=[s32k, ("eb", par), "stbf"], writes=[s32k])
                            S.op("vector", lambda e, sv=sv, bank=bank: e.tensor_tensor(out=sv, in0=sv, in1=B[bank][:, :].rearrange("p (s v) -> p s v", s=4), op=ALU.add),
                                 reads=[s32k, ("B", bank)], writes=[s32k])
                        else:
                            S.op("vector", lambda e, sv=sv, bank=bank: e.scalar_tensor_tensor(out=sv, in0=sv, scalar=decays[hh][1], in1=B[bank][:, :].rearrange("p (s v) -> p s v", s=4), op0=ALU.mult, op1=ALU.add),
                                 reads=[s32k, ("B", bank), "stbf"], writes=[s32k])
                    for q4 in range(4):
                        dst = (hss if is_h else rss)[q4 * 4:(q4 + 1) * 4, hh, :, :].rearrange("s d v -> d s v")
                        S.dma(lambda e, dst=dst, q4=q4: e.dma_start(out=dst, in_=s32[:, q4 * 4:(q4 + 1) * 4, :]), reads=[s32k])
                    if h + 1 < 8:
                        load_states(h + 1)
                yield

            def load_states(h):
                for q4 in range(4):
                    src = (sth if h < 4 else strr)[q4 * 4:(q4 + 1) * 4, h % 4, :, :].rearrange("s d v -> d s v")
                    S.dma(lambda e, src=src, q4=q4: e.dma_start(out=st32[0][:, q4 * 4:(q4 + 1) * 4, :], in_=src), writes=[("st32", 0)])

            items = [(h, bi) for h in range(8) for bi in range(5)]
            gctr = [0]
            load_head_weights(0)
            load_states(0)

            def run_slot(gens):
                gens = [g_ for g_ in gens if g_ is not None]
                for g_ in list(gens):
                    if getattr(g_, "gi_code", None) is not None and g_.gi_code.co_name in ("gen_elem", "elem_item"):
                        try:
                            while next(g_) != "EVAC_DONE":
                                pass
                        except StopIteration:
                            gens.remove(g_)
                while gens:
                    for g_ in list(gens):
                        try:
                            next(g_)
                        except StopIteration:
                            gens.remove(g_)

            def elem_item(i):
                h, bi = items[i]
                yield from gen_elem(h, bi, i % 3, i % 3)

            cst_state = {"gc": 0, "sp": 0, "gc0": 0}
            nit = len(items)

            def seq_item(j):
                if 0 <= j - 2 < nit:
                    yield from gen_tail(items[j - 2][0], items[j - 2][1], (j - 2) % 3, (j - 2) % 3, cst_state)
                    chk("q%da" % j)
                if 0 <= j - 1 < nit:
                    yield from gen_pre(items[j - 1][0], items[j - 1][1], (j - 1) % 3)
                    chk("q%db" % j)
                    yield from gen_chain(items[j - 1][0], items[j - 1][1], (j - 1) % 3, cst_state)

            def seq_item2(j):
                if 0 <= j - 1 < nit:
                    yield from gen_pre(items[j - 1][0], items[j - 1][1], (j - 1) % 3)
                    yield from gen_chain(items[j - 1][0], items[j - 1][1], (j - 1) % 3, cst_state)

            for j in range(-1, nit + 2):
                gens = []
                if 0 <= j - 2 < nit:
                    run_slot([gen_tail(items[j - 2][0], items[j - 2][1], (j - 2) % 3, (j - 2) % 3, cst_state)])
                if 0 <= j < nit:
                    gens.append(elem_item(j))
                if j + 1 < nit:
                    gens.append(gen_proj(*items[j + 1]))
                gens.append(seq_item2(j))
                run_slot(gens)
                chk("s%d" % j)
        flush_casts()
        chk("p1")
        S.barrier()

        with contextlib.ExitStack() as p2:
            def sb2(name, shape, dt=F32):
                return sb(name, shape, dt, st=p2)

            acc = sb("acc", [128, 17, D])
            lng = sb2("lng", [128, D])
            lnbt = sb2("lnbt", [128, D])
            xst2 = [sb2("xst2_%d" % i, [128, D]) for i in range(2)]
            rr = [sb2("rr%d" % i, [128, D]) for i in range(2)]
            xb2 = [sb2("xb2_%d" % i, [128, D], BF16) for i in range(3)]
            stats = sb2("stats", [128, 2, 6])
            mv = sb2("mv", [128, 2])
            lnv = sb2("lnv", [128, 1])
            rstd = sb2("rstd2", [128, 1])

            S.dma(lambda e, lng=lng: e.dma_start(out=lng[:], in_=lnb[0]), writes=["lng"])
            S.dma(lambda e, lnbt=lnbt: e.dma_start(out=lnbt[:], in_=lnb[1]), writes=["lnbt"])

            def layer_norm(r, rows, dst, dst_key, rkey, bufs):
                stats_, mv_, lnv_, rstd_, lng_, lnbt_ = bufs
                for c in range(2):
                    S.op("vector", lambda e, c=c: e.bn_stats(out=stats_[0:rows, c, :], in_=r[0:rows, c * 512:(c + 1) * 512]), reads=[rkey], writes=["stats"])
                S.op("vector", lambda e: e.bn_aggr(out=mv_[0:rows, :], in_=stats_[0:rows, :, :].rearrange("p a b -> p (a b)")), reads=["stats"], writes=["mv"])
                act(lnv_[0:rows, :], mv_[0:rows, 1:2], AF.Ln, ["mv", "cst"], ["lnv"], bias=C("eps")[0:rows, :])
                act(rstd_[0:rows, :], lnv_[0:rows, :], AF.Exp, ["lnv"], ["rstd2"], scale=-0.5)
                nmr_ = lnv_
                S.op("vector", lambda e: e.tensor_scalar(out=nmr_[0:rows, :], in0=mv_[0:rows, 0:1], scalar1=rstd_[0:rows, 0:1], scalar2=-1.0, op0=ALU.mult, op1=ALU.mult),
                     reads=["mv", "rstd2", "lnv"], writes=["lnv"])
                act(r[0:rows, :], r[0:rows, :], AF.Identity, [rkey, "rstd2", "lnv"], [rkey], scale=rstd_[0:rows, 0:1], bias=nmr_[0:rows, 0:1])
                tt("vector", r[0:rows, :], r[0:rows, :], lng_[0:rows, :], ALU.mult, [rkey, "lng"], [rkey])
                tt("gpsimd", dst, r[0:rows, :], lnbt_[0:rows, :], ALU.add, [rkey, "lnbt"], [dst_key])

            ln_bufs1 = (stats, mv, lnv, rstd, lng, lnbt)

            def p1b_mm(t):
                rows = 128 if t < 16 else 64
                bi = min(t // 4, 4)
                src = xp[t * 128:(t + 1) * 128, :] if t < 16 else xs
                sl = t % 2
                tc0 = t * 128
                S.dma(lambda e: e.dma_start(out=xst2[sl][0:rows, :], in_=src), writes=[("xst2", sl)])
                for half in range(2):
                    bank = 2 * sl + half
                    for hh in range(8):
                        S.op("tensor", lambda e, hh=hh, half=half, bank=bank: e.matmul(B[bank][:, :], lhsT=mixT[:, hh, tc0:tc0 + 128], rhs=wout[:, hh, half * 512:(half + 1) * 512],
                                                                                    start=(hh == 0), stop=(hh == 7)),
                             reads=[("mixT", bi), ("wout", hh)], writes=[("B", bank)])

            def p1b_post(t):
                rows = 128 if t < 16 else 64
                bi = min(t // 4, 4)
                sl = t % 2
                tc0 = t * 128
                for half in range(2):
                    bank = 2 * sl + half
                    S.op("vector", lambda e, half=half, bank=bank: e.scalar_tensor_tensor(out=rr[sl][0:rows, half * 512:(half + 1) * 512], in0=xst2[sl][0:rows, half * 512:(half + 1) * 512],
                                                                                         scalar=ALPHA, in1=B[bank][0:rows, :], op0=ALU.mult, op1=ALU.add),
                         reads=[("xst2", sl), ("B", bank)], writes=[("rr", sl)])
                layer_norm(rr[sl], rows, acc[0:rows, t, :], ("acc", t), ("rr", sl), ln_bufs1)
                if rows < 128:
                    S.op("gpsimd", lambda e: e.memset(xb2[t % 3][rows:128, :], 0.0), writes=[("xb2", t % 3)])
                act(xb2[t % 3][0:rows, :], acc[0:rows, t, :], AF.Copy, [("acc", t)], [("xb2", t % 3)])

            def p1b_tr(t):
                rows = 128 if t < 16 else 64
                transpose_tile(xb2[t % 3], rows, t * 128, 4 + t % 2, t, inT, ("inT", min(t // 4, 4)), ("xb2", t % 3))

            p1b_mm(0)
            for t in range(17):
                if t + 1 < 17:
                    p1b_mm(t + 1)
                p1b_post(t)
                if t >= 1:
                    p1b_tr(t - 1)
            p1b_tr(16)
            chk("p1b")
            S.barrier()

        with contextlib.ExitStack() as p3:
            def sb3(name, shape, dt=F32):
                return sb(name, shape, dt, st=p3)

            lng = sb3("lng3", [128, D])
            lnbt = sb3("lnbt3", [128, D])
            wgu = [sb3("wgu%d" % i, [128, 2, 8, 128], BF16) for i in range(2)]
            hT = mixT
            sg = [sb3("sg%d" % i, [128, 512]) for i in range(2)]
            yy = [sb3("yy%d" % i, [128, D]) for i in range(2)]
            stats = sb3("stats3", [128, 2, 6])
            mv = sb3("mv3", [128, 2])
            lnv = sb3("lnv3", [128, 1])
            rstd = sb3("rstd3", [128, 1])

            ln_bufs3 = (stats, mv, lnv, rstd, lng, lnbt)
            S.dma(lambda e, lng=lng: e.dma_start(out=lng[:], in_=lnb[2]), writes=["lng"])
            S.dma(lambda e, lnbt=lnbt: e.dma_start(out=lnbt[:], in_=lnb[3]), writes=["lnbt"])
            fctr = 0
            pctr = 0
            for gi, grp in enumerate(FGROUPS):
                last = gi == len(FGROUPS) - 1
                ds = gi % 2
                for fi, f in enumerate(grp):
                    load_cast(w_down[f * 128:(f + 1) * 128, :], wout[:, ds * 4 + fi, :], ("wd", ds, fi))
                for fi, f in enumerate(grp):
                    ws = fctr % 2
                    fctr += 1
                    load_cast(w_gate_v[:, :, f * 128:(f + 1) * 128], wgu[ws][:, 0, :, :], ("wgu", ws, 0), shape3=(8, 128))
                    load_cast(w_up_v[:, :, f * 128:(f + 1) * 128], wgu[ws][:, 1, :, :], ("wgu", ws, 1), shape3=(8, 128))
                    for bi, (c0, N, kind) in enumerate(BLOCKS):
                        pb = pctr % 2
                        pctr += 1
                        for which in range(2):
                            bank = 2 * pb + which
                            for k in range(8):
                                S.op("tensor", lambda e, k=k, which=which, bank=bank, c0=c0, N=N, ws=ws: e.matmul(B[bank][:, 0:N], lhsT=wgu[ws][:, which, k, :], rhs=inT[:, k, c0:c0 + N],
                                                                                                             start=(k == 0), stop=(k == 7)),
                                     reads=[("wgu", ws, which), ("inT", bi)], writes=[("B", bank)])
                        act(sg[pb][:, 0:N], B[2 * pb][:, 0:N], AF.Silu, [("B", 2 * pb)], [("sg", pb)])
                        tt("vector", hT[:, fi, c0:c0 + N], sg[pb][:, 0:N], B[2 * pb + 1][:, 0:N], ALU.mult, [("sg", pb), ("B", 2 * pb + 1)], [("hT", fi, bi)])
                for t in range(17):
                    rows = 128 if t < 16 else 64
                    bi = min(t // 4, 4)
                    tc0 = t * 128
                    sl = t % 2
                    for half in range(2):
                        bank = 4 + 2 * sl + half
                        for fi in range(len(grp)):
                            S.op("tensor", lambda e, fi=fi, half=half, bank=bank, tc0=tc0, rows=rows, ds=ds, ng=len(grp): e.matmul(B[bank][:, :], lhsT=hT[:, fi, tc0:tc0 + 128], rhs=wout[:, ds * 4 + fi, half * 512:(half + 1) * 512],
                                                                                                          start=(fi == 0), stop=(fi == ng - 1)),
                                 reads=[("hT", fi, bi), ("wd", ds, fi)], writes=[("B", bank)])
                        hs = slice(half * 512, (half + 1) * 512)
                        dst = yy[sl][0:rows, hs] if last else acc[0:rows, t, hs]
                        dkey = ("yy", sl) if last else ("acc", t)
                        if gi == 0:
                            S.op("vector", lambda e, hs=hs, bank=bank, t=t, rows=rows, dst=dst: e.scalar_tensor_tensor(out=dst, in0=acc[0:rows, t, hs], scalar=ALPHA, in1=B[bank][0:rows, :], op0=ALU.mult, op1=ALU.add),
                                 reads=[("acc", t), ("B", bank)], writes=[dkey])
                        else:
                            tt("vector", dst, acc[0:rows, t, hs], B[bank][0:rows, :], ALU.add, [("acc", t), ("B", bank)], [dkey])
                    if last:
                        layer_norm(yy[sl], rows, yy[sl][0:rows, :], ("yy", sl), ("yy", sl), ln_bufs3)
                        odst = yp[t * 128:(t + 1) * 128, :] if t < 16 else ys
                        S.dma(lambda e, odst=odst, sl=sl, rows=rows: e.dma_start(out=odst, in_=yy[sl][0:rows, :]), reads=[("yy", sl)])
        S.emit()
    return nc


_NC_CACHE = {}


def kernel(x_prompt, x_sample, state_hgrn, state_ret, w_in, lb_logits, hgrn_norm_g, ret_norm_g,
           w_out, ln1_g, ln1_b, w_gate, w_up, w_down, ln2_g, ln2_b):
    f = lambda a: np.ascontiguousarray(np.asarray(a, dtype=np.float32))
    x_prompt, x_sample, state_hgrn, state_ret = f(x_prompt), f(x_sample), f(state_hgrn), f(state_ret)
    packed, offs, ropeq, ropek, _ = _CONSTS
    if "nc" not in _NC_CACHE:
        _NC_CACHE["nc"] = build_nc()
    nc = _NC_CACHE["nc"]
    lbl = f(f(lb_logits).reshape(2, 4, 128).transpose(2, 0, 1).reshape(128, 8))
    gcol = f(np.concatenate([f(hgrn_norm_g)[0], f(ret_norm_g)[0]], 0).T)
    lnb = f(np.stack([np.broadcast_to(f(a)[0], (128, D)) for a in (ln1_g, ln1_b, ln2_g, ln2_b)]))
    shared = dict(w_in=f(w_in)[0], w_out=f(w_out)[0], w_gate=f(w_gate)[0], w_up=f(w_up)[0], w_down=f(w_down)[0],
                  lbl=lbl, gcol=gcol, lnb=lnb, cst=packed, ropeq=ropeq, ropek=ropek)
    in_maps = []
    for c in range(8):
        m = dict(shared)
        m["xp"] = x_prompt[c]
        m["xs"] = f(x_sample[16 * c:16 * c + 16].reshape(NS, D))
        m["sth"] = f(state_hgrn[0, 16 * c:16 * c + 16])
        m["str"] = f(state_ret[0, 16 * c:16 * c + 16])
        in_maps.append(m)
    res = run_bass_kernel_spmd(nc, in_maps, core_ids=list(range(8)))
    r = res.results
    y_prompt = np.stack([r[c]["yp"] for c in range(8)], 0)
    y_sample = np.concatenate([r[c]["ys"].reshape(16, 4, D) for c in range(8)], 0)
    hsp = np.stack([r[c]["hsp"] for c in range(8)], 0)[None]
    rsp = np.stack([r[c]["rsp"] for c in range(8)], 0)[None]
    hss = np.concatenate([r[c]["hss"] for c in range(8)], 0)[None]
    rss = np.concatenate([r[c]["rss"] for c in range(8)], 0)[None]
    return (y_prompt.astype(np.float32), y_sample.astype(np.float32), hsp.astype(np.float32), rsp.astype(np.float32),
            hss.astype(np.float32), rss.astype(np.float32))
```

```python
import contextlib
import math
import numpy as np
import concourse.bass as bass
import concourse.mybir as mybir
from concourse.bass_utils import run_bass_kernel_spmd

F32 = mybir.dt.float32
BF16 = mybir.dt.bfloat16
ALU = mybir.AluOpType
AF = mybir.ActivationFunctionType

D = 1024
T = 2048
NSQ = 16
NS = 64
NT = T + 128
DFF = 2816
NFC = DFF // 128
ALPHA = 2.0 ** 0.25
EPS = 1e-5
SCALE = 128.0 ** -0.5
BLOCKS = [(0, 512, "p"), (512, 512, "p"), (1024, 512, "p"), (1536, 512, "p"), (2048, 128, "s")]
FGROUPS = [[0, 1, 2, 3], [4, 5, 6, 7], [8, 9, 10, 11], [12, 13, 14, 15], [16, 17, 18], [19, 20, 21]]
ENGINES = ("tensor", "vector", "scalar", "gpsimd", "sync")


class Sched:
    def __init__(self, nc, n_dma_sems=4):
        self.nc = nc
        self.ops = []
        self.last_writer = {}
        self.readers = {}
        self.n_dma_sems = n_dma_sems
        self.pending_barrier = {}

    stopped = False

    def op(self, eng, fn, reads=(), writes=(), dma=False):
        if self.stopped:
            return None
        idx = len(self.ops)
        deps = set()
        for r in reads:
            w = self.last_writer.get(r)
            if w is not None:
                deps.add(w)
        for w in writes:
            lw = self.last_writer.get(w)
            if lw is not None:
                deps.add(lw)
            for rd in self.readers.get(w, ()):
                deps.add(rd)
        pb = self.pending_barrier.pop(eng, None)
        if pb:
            deps |= pb
        deps.discard(idx)
        self.ops.append(dict(eng=eng, fn=fn, deps=deps, dma=dma, signal=False))
        for r in reads:
            self.readers.setdefault(r, []).append(idx)
        for w in writes:
            self.last_writer[w] = idx
            self.readers[w] = []
        return idx

    def dma(self, fn, reads=(), writes=(), queue="sync"):
        return self.op(queue, fn, reads, writes, dma=True)

    def barrier(self):
        deps = set()
        last = {}
        for i, o in enumerate(self.ops):
            if o["dma"]:
                deps.add(i)
            else:
                last[o["eng"]] = i
        deps |= set(last.values())
        for e in ENGINES:
            self.pending_barrier[e] = set(deps) | self.pending_barrier.get(e, set())

    def emit(self, final_wait_engine="sync"):
        nc = self.nc
        ops = self.ops
        for i, o in enumerate(ops):
            for d in o["deps"]:
                p = ops[d]
                if p["dma"] or p["eng"] != o["eng"] or o["eng"] != "tensor":
                    p["signal"] = True
        for o in ops:
            if o["dma"]:
                o["signal"] = True
        cnt = {e: 0 for e in ENGINES}
        dma_cnt = {e: 0 for e in ENGINES}
        dma_val = {}
        for i, o in enumerate(ops):
            if not o["signal"]:
                continue
            if o["dma"]:
                q = o["eng"]
                k = dma_cnt[q] % self.n_dma_sems
                dma_cnt[q] += 1
                key = (q, k)
                prev = dma_val.get(key, 0)
                o["prev_tick"] = (key, prev) if prev else None
                dma_val[key] = prev + 16
                o["tick"] = (key, prev + 16)
            else:
                cnt[o["eng"]] += 1
                o["tick"] = (o["eng"], cnt[o["eng"]])
        with contextlib.ExitStack() as st:
            sems = {}
            for e in ENGINES:
                sems[e] = st.enter_context(nc.semaphore("s_" + e))
            for key in dma_val:
                sems[key] = st.enter_context(nc.semaphore("d_%s_%d" % key))
            block = st.enter_context(nc.Block())
            for e in ENGINES:
                my = [(i, o) for i, o in enumerate(ops) if o["eng"] == e]
                if not my and e != final_wait_engine:
                    continue

                def body(eng, e=e, my=my):
                    seen = {}
                    for i, o in my:
                        need = {}
                        for d in o["deps"]:
                            p = ops[d]
                            if not p["signal"]:
                                continue
                            if (not p["dma"]) and p["eng"] == e and e == "tensor":
                                continue
                            k, v = p["tick"]
                            if v > need.get(k, 0):
                                need[k] = v
                        if o["dma"] and o.get("prev_tick"):
                            k, v = o["prev_tick"]
                            if v > need.get(k, 0):
                                need[k] = v
                        for k, v in need.items():
                            if seen.get(k, 0) >= v:
                                continue
                            eng.wait_ge(sems[k], v)
                            seen[k] = v
                        ins = o["fn"](eng)
                        if o["signal"]:
                            if o["dma"]:
                                ins.then_inc(sems[o["tick"][0]], 16)
                            else:
                                ins.then_inc(sems[e], 1)
                    if e == final_wait_engine:
                        for key, v in dma_val.items():
                            eng.wait_ge(sems[key], v)

                getattr(block, e)(body)
        return cnt


def _consts():
    c = {}
    p = np.arange(128)
    gam = 1.0 - 2.0 ** (-5.0 - np.arange(4, dtype=np.float64))
    j = p[:, None]
    i = p[None, :]
    c["maskH"] = ((j // 64 == i // 64) & (i >= j)).astype(np.float64)
    j6 = np.arange(64)[:, None]
    i6 = np.arange(64)[None, :]
    mS = ((j6 // 4 == i6 // 4) & (i6 >= j6))
    mh = np.zeros((128, 128))
    mh[:64, :64] = mS
    c["maskHS"] = mh
    for h in range(4):
        m = np.where(i >= j, SCALE * gam[h] ** (-(j + 1.0)), 0.0)
        c["maskR%d" % h] = m
        ms = np.zeros((128, 128))
        ms[:64, :64] = np.where(mS, SCALE * gam[h] ** (-((j6 % 4) + 1.0)), 0.0)
        c["maskRS%d" % h] = ms
    ksc = np.zeros((128, 8))
    for h in range(4):
        ksc[:, h] = SCALE * gam[h] ** (127.0 - p)
        ksc[:64, 4 + h] = SCALE * gam[h] ** (3.0 - (np.arange(64) % 4))
    c["ksc"] = ksc
    sm = np.zeros((128, 16))
    sm[:64] = (np.arange(64)[:, None] // 4 == np.arange(16)[None, :])
    c["seqmask"] = sm
    scm = np.ones((128, 512))
    scm[:, ::64] = 0.0
    c["scanm"] = scm
    scs = np.ones((128, 128))
    scs[:, ::4] = 0.0
    c["scanmS"] = scs
    c["ident"] = np.eye(128)
    c["ones"] = np.ones((128, 128))
    c["eps"] = np.full((128, 1), EPS)
    c["one"] = np.ones((128, 1))
    c["rm0"] = (p < 64).astype(np.float64)[:, None]
    c["rm1"] = (p >= 64).astype(np.float64)[:, None]
    offs = {}
    o = 0
    arrs = []
    for k, v in c.items():
        offs[k] = (o, v.shape[1])
        o += v.shape[1]
        arrs.append(v)
    packed = np.concatenate(arrs, axis=1).astype(np.float32)
    pos = np.concatenate([np.arange(T, dtype=np.float64), 16384.0 + (np.arange(128) % 4)])
    tl = np.concatenate([np.arange(T) % 128, np.arange(128) % 4]).astype(np.float64)
    inv = 10000.0 ** (-(np.arange(64, dtype=np.float64)) / 64.0)
    ang = inv[p % 64][:, None] * pos[None, :]
    sgn = np.where(p < 64, -1.0, 1.0)[:, None]
    cos = np.cos(ang)
    sin = np.sin(ang) * sgn
    ropek = np.stack([cos, sin]).astype(np.float32)
    ropeq = np.zeros((4, 2, 128, NT), np.float32)
    for h in range(4):
        g = gam[h] ** (tl + 1.0)
        ropeq[h, 0] = cos * g[None, :]
        ropeq[h, 1] = sin * g[None, :]
    decays = [(float(gam[h] ** 128.0), float(gam[h] ** 4.0)) for h in range(4)]
    return packed, offs, ropeq, ropek, decays


_CONSTS = _consts()


def build_nc(stop_at=None):
    packed, offs, _, _, decays = _CONSTS
    NCOL = packed.shape[1]
    nc = bass.Bass("TRN2", target_bir_lowering=False)

    def din(name, shape):
        return nc.dram_tensor(name, list(shape), F32, kind="ExternalInput").ap()

    def dout(name, shape):
        return nc.dram_tensor(name, list(shape), F32, kind="ExternalOutput").ap()

    xp = din("xp", [T, D])
    xs = din("xs", [NS, D])
    sth = din("sth", [NSQ, 4, 128, 128])
    strr = din("str", [NSQ, 4, 128, 128])
    w_in = din("w_in", [D, 4096])
    w_out = din("w_out", [D, D])
    w_gate = din("w_gate", [D, DFF])
    w_up = din("w_up", [D, DFF])
    w_down = din("w_down", [DFF, D])
    lbl = din("lbl", [128, 8])
    gcol_d = din("gcol", [128, 8])
    lnb = din("lnb", [4, 128, D])
    cst_d = din("cst", [128, NCOL])
    ropeq_d = din("ropeq", [4, 2, 128, NT])
    ropek_d = din("ropek", [2, 128, NT])

    yp = dout("yp", [T, D])
    ys = dout("ys", [NS, D])
    hsp = dout("hsp", [4, 128, 128])
    rsp = dout("rsp", [4, 128, 128])
    hss = dout("hss", [NSQ, 4, 128, 128])
    rss = dout("rss", [NSQ, 4, 128, 128])

    w_in_v = w_in.rearrange("(k p) c -> p k c", p=128)
    w_gate_v = w_gate.rearrange("(k p) c -> p k c", p=128)
    w_up_v = w_up.rearrange("(k p) c -> p k c", p=128)

    S = Sched(nc)

    def chk(tag):
        if stop_at is not None and tag == stop_at:
            S.stopped = True

    with contextlib.ExitStack() as g:
        def sb(name, shape, dt=F32, st=g):
            return st.enter_context(nc.sbuf_tensor(name, list(shape), dt))

        B = [g.enter_context(nc.psum_tensor("B%d" % i, [128, 512], F32)) for i in range(8)]
        Bbf = [b.bitcast(BF16) for b in B]
        inT = sb("inT", [128, 8, NT], BF16)
        cst = sb("cst_sb", [128, NCOL])
        ident = sb("ident_bf", [128, 128], BF16)
        ones = sb("ones_bf", [128, 128], BF16)
        NST = 3
        wst = [sb("wst%d" % i, [128, 1024]) for i in range(NST)]
        wst_ctr = [0]

        def C(name, rows=128):
            o, w = offs[name]
            return cst[0:rows, o:o + w]

        S.dma(lambda e: e.dma_start(out=cst[:], in_=cst_d), writes=["cst"])
        S.op("gpsimd", lambda e: e.tensor_copy(out=ident[:], in_=C("ident")), reads=["cst"], writes=["ident"])
        S.op("gpsimd", lambda e: e.tensor_copy(out=ones[:], in_=C("ones")), reads=["cst"], writes=["ones"])

        pending_casts = []

        def flush_casts():
            while pending_casts:
                pending_casts.pop(0)()

        def load_cast(src, dst, dst_key, shape3=None, scale=None, scale_key=None, defer=False):
            slot = wst_ctr[0] % NST
            wst_ctr[0] += 1
            stg = wst[slot]
            if shape3 is not None:
                sv = stg[:].rearrange("p (a b) -> p a b", a=shape3[0])
            else:
                sv = stg[:]
            S.dma(lambda e: e.dma_start(out=sv, in_=src), writes=[("wst", slot)])
            if defer:
                pending_casts.append(lambda: S.op("vector", lambda e: e.tensor_copy(out=dst, in_=sv), reads=[("wst", slot)], writes=[dst_key]))
                return
            if scale is None:
                S.op("gpsimd", lambda e: e.tensor_copy(out=dst, in_=sv), reads=[("wst", slot)], writes=[dst_key])
            else:
                S.op("gpsimd", lambda e: e.tensor_scalar(out=dst, in0=sv, scalar1=scale, scalar2=None, op0=ALU.mult),
                     reads=[("wst", slot), scale_key], writes=[dst_key])

        with contextlib.ExitStack() as p1:
            def sb1(name, shape, dt=F32):
                return sb(name, shape, dt, st=p1)

            mixT = sb("mixT", [128, 8, NT], BF16)
            wout = sb("wout", [128, 8, D], BF16)
            gcol = sb("gcol_sb", [128, 8])
            S.dma(lambda e: e.dma_start(out=gcol[:], in_=gcol_d), writes=["gcol"])

            def load_wout(which=range(8)):
                for hh_ in which:
                    slot = wst_ctr[0] % NST
                    wst_ctr[0] += 1
                    S.dma(lambda e, slot=slot, hh_=hh_: e.dma_start(out=wst[slot][:], in_=w_out[hh_ * 128:(hh_ + 1) * 128, :]), writes=[("wst", slot)])
                    pending_casts.append(lambda slot=slot, hh_=hh_: S.op(
                        "vector", lambda e: e.tensor_scalar(out=wout[:, hh_, :], in0=wst[slot][:], scalar1=gcol[:, hh_:hh_ + 1], scalar2=None, op0=ALU.mult),
                        reads=[("wst", slot), "gcol"], writes=[("wout", hh_)]))
            wh = [sb1("wh%d" % i, [128, 4, 8, 128], BF16) for i in range(2)]
            lbt = sb1("lbt", [128, 8])
            lbd = sb1("lbd", [128, 4])
            lb = sb1("lb", [128, 4])
            oml = sb1("oml", [128, 4])
            noml = sb1("noml", [128, 4])
            tf = {n: sb1("t_" + n, [128, 512]) for n in ["a0", "a1", "a2", "a3", "a4", "a5", "a6", "std", "rstd", "on"]}
            for a, (n1, n2) in enumerate([("q", "qs"), ("sig", "ks"), ("logf", "t1"), ("k", "t2"), ("b", "t3"), ("enb", "t4"), ("kt", "kt_")]):
                tf[n1] = tf["a%d" % a]
                tf[n2] = tf["a%d" % a]
            TK = {"q": "a0", "qs": "a0", "sig": "a1", "ks": "a1", "logf": "a2", "t1": "a2", "k": "a3", "t2": "a3", "b": "a4", "t3": "a4",
                  "enb": "a5", "t4": "a5", "kt": "a6"}
            tabs = [sb1("tab%d" % i, [128, 512]) for i in range(4)]
            gate = [sb1("gate%d" % i, [128, 512]) for i in range(3)]
            ebb = [sb1("eb%d" % i, [128, 512]) for i in range(3)]
            qt = [sb1("qt%d" % i, [128, 512], BF16) for i in range(3)]
            ktb = [sb1("ktb%d" % i, [128, 512], BF16) for i in range(3)]
            kh = [sb1("kh%d" % i, [128, 512], BF16) for i in range(3)]
            vbf = [sb1("vbf%d" % i, [128, 512], BF16) for i in range(3)]
            sq = sb1("sq", [128, 512], BF16)
            khtok = sb1("khtok", [128, 4, 128], BF16)
            khtok2 = sb1("khtok2", [128, 4, 128], BF16)
            atm = sb1("atm", [128, 4, 128], BF16)
            khm = sb1("khm", [128, 16, 128], BF16)
            NSB = 16
            Sbf = sb1("Sbf", [128, NSB, 128], BF16)
            S32 = sb1("S32", [128, 2, 128])
            st32 = [sb1("st32_0", [128, 16, 128])]
            stbf = sb1("stbf", [128, 16, 128], BF16)
            xst = [st32[0][:, 8 * i:8 * (i + 1), :].rearrange("p a b -> p (a b)") for i in range(2)]
            xbf = [stbf[:, 8 * i:8 * (i + 1), :].rearrange("p a b -> p (a b)") for i in range(2)]

            S.dma(lambda e: e.dma_start(out=lbt[:], in_=lbl), writes=["lbt"])
            S.op("vector", lambda e: e.tensor_tensor(out=lbd[:], in0=lbt[:, 0:4], in1=lbt[:, 4:8], op=ALU.subtract), reads=["lbt"], writes=["lbd"])
            S.op("scalar", lambda e: e.activation(out=lb[:], in_=lbd[:], func=AF.Sigmoid), reads=["lbd"], writes=["lb"])
            S.op("vector", lambda e: e.tensor_scalar(out=oml[:], in0=lb[:], scalar1=-1.0, scalar2=1.0, op0=ALU.mult, op1=ALU.add), reads=["lb"], writes=["oml"])
            S.op("vector", lambda e: e.tensor_scalar(out=noml[:], in0=lb[:], scalar1=-1.0, scalar2=None, op0=ALU.add), reads=["lb"], writes=["noml"])

            def transpose_tile(src_bf, rows, tcol, pbank, ci, dst, dst_key, src_key):
                pv = Bbf[pbank][:, 0:1024].rearrange("p (k c) -> p k c", k=8)
                for k in range(8):
                    S.op("tensor", lambda e, k=k: e.transpose(out=pv[:, k, :], in_=src_bf[:, k * 128:(k + 1) * 128], identity=ident[:, :]),
                         reads=[src_key, "ident"], writes=[("B", pbank)])
                if ci % 2 == 0:
                    S.op("scalar", lambda e: e.activation(out=dst[:, :, tcol:tcol + 128], in_=pv[:, :, :], func=AF.Copy),
                         reads=[("B", pbank)], writes=[dst_key])
                else:
                    S.op("vector", lambda e: e.tensor_copy(out=dst[:, :, tcol:tcol + 128], in_=pv[:, :, :]),
                         reads=[("B", pbank)], writes=[dst_key])

            for t in range(17):
                rows = 128 if t < 16 else 64
                src = xp[t * 128:(t + 1) * 128, :] if t < 16 else xs
                sl = t % 2
                if rows < 128:
                    S.op("gpsimd", lambda e, sl=sl, rows=rows: e.memset(xst[sl][rows:128, :], 0.0), writes=[("xst", sl)])
                S.dma(lambda e, sl=sl, rows=rows, src=src: e.dma_start(out=xst[sl][0:rows, :], in_=src), writes=[("xst", sl)])
                if t % 2 == 0:
                    S.op("vector", lambda e, sl=sl: e.tensor_copy(out=xbf[sl][:, :], in_=xst[sl][:, :]), reads=[("xst", sl)], writes=[("xbf", sl)])
                else:
                    S.op("gpsimd", lambda e, sl=sl: e.tensor_copy(out=xbf[sl][:, :], in_=xst[sl][:, :]), reads=[("xst", sl)], writes=[("xbf", sl)])
                transpose_tile(xbf[sl], rows, t * 128, 4 + sl, t, inT, ("inT", min(t // 4, 4)), ("xbf", sl))
            S.barrier()

            chk("p0")
            def load_head_weights(h, secs=(0, 1, 2, 3)):
                slot = h % 2
                base = 0 if h < 4 else 2048
                hh = h % 4
                for sec in secs:
                    c0 = base + sec * 512 + hh * 128
                    load_cast(w_in_v[:, :, c0:c0 + 128], wh[slot][:, sec, :, :], ("wh", slot, sec), shape3=(8, 128), defer=(len(secs) == 1))

            def proj_fm(h, bi, sec, bank):
                c0, N, _ = BLOCKS[bi]
                slot = h % 2
                for k in range(8):
                    S.op("tensor", lambda e, k=k: e.matmul(B[bank][:, 0:N], lhsT=wh[slot][:, sec, k, :], rhs=inT[:, k, c0:c0 + N], start=(k == 0), stop=(k == 7)),
                         reads=[("wh", slot, sec), ("inT", bi)], writes=[("B", bank)])
                    if k % 4 == 3:
                        yield

            def proj_tm(h, bi, sec, bank):
                c0, N, kind = BLOCKS[bi]
                slot = h % 2
                rows = 128
                for t in range(max(N // 128, 1)):
                    for k in range(8):
                        S.op("tensor", lambda e, k=k, t=t: e.matmul(B[bank][0:rows, t * 128:(t + 1) * 128], lhsT=inT[:, k, c0 + t * 128:c0 + t * 128 + rows],
                                                                     rhs=wh[slot][:, sec, k, :], start=(k == 0), stop=(k == 7)),
                             reads=[("wh", slot, sec), ("inT", bi)], writes=[("B", bank)])
                    yield

            def gen_proj(h, bi):
                flush_casts()
                if bi < 4 and h + 1 < 8:
                    load_head_weights(h + 1, secs=(bi,))
                if h == 3 and bi < 4:
                    load_wout(which=(2 * bi, 2 * bi + 1))
                yield from proj_fm(h, bi, 3, 2)
                yield from proj_fm(h, bi, 0, 0)
                yield from proj_tm(h, bi, 2, 3)
                yield from proj_fm(h, bi, 1, 1)

            def act(out, in_, func, reads, writes, **kw):
                S.op("scalar", lambda e: e.activation(out=out, in_=in_, func=func, **kw), reads=reads, writes=writes)

            def tt(eng, out, in0, in1, op, reads, writes):
                S.op(eng, lambda e: e.tensor_tensor(out=out, in0=in0, in1=in1, op=op), reads=reads, writes=writes)

            def gen_elem(h, bi, par, gi):
                c0, N, kind = BLOCKS[bi]
                is_h = h < 4
                hh = h % 4
                smp = kind == "s"
                V = lambda n: tf[n][:, 0:N]
                K_ = lambda n: TK[n]
                G = gate[gi][:, 0:N]
                EB = ebb[par][:, 0:N]
                act(G, B[2][:, 0:N], AF.Silu, [("B", 2)], [("gate", gi)])
                if is_h:
                    act(V("q"), B[0][:, 0:N], AF.Silu, [("B", 0)], [K_("q")])
                    act(vbf[par][:, 0:N], B[3][:, 0:N], AF.Copy, [("B", 3)], [("vbf", par)])
                    act(V("sig"), B[1][:, 0:N], AF.Sigmoid, [("B", 1)], [K_("sig")])
                else:
                    for ti, src in enumerate([ropeq_d[hh, 0], ropeq_d[hh, 1], ropek_d[0], ropek_d[1]]):
                        S.dma(lambda e, ti=ti, src=src: e.dma_start(out=tabs[ti][:, 0:N], in_=src[:, c0:c0 + N]), writes=[("tab", ti)])
                    for nm, bank, tc, ta in (("qs", 0, 0, "t1"), ("ks", 1, 2, "t3")):
                        act(tf[nm][0:64, 0:N], B[bank][64:128, 0:N], AF.Copy, [("B", bank)], [K_(nm) + "lo"])
                        act(tf[nm][64:128, 0:N], B[bank][0:64, 0:N], AF.Copy, [("B", bank)], [K_(nm) + "hi"])
                        tt("vector", V(ta), B[bank][:, 0:N], tabs[tc][:, 0:N], ALU.mult, [("B", bank), ("tab", tc)], [K_(ta)])
                        if bank == 0:
                            act(vbf[par][:, 0:N], B[3][:, 0:N], AF.Copy, [("B", 3)], [("vbf", par)])
                yield "EVAC_DONE"
                if is_h:
                    act(V("logf"), V("sig"), AF.Ln, [K_("sig"), "oml", "lb"], [K_("logf")], scale=oml[:, hh:hh + 1], bias=lb[:, hh:hh + 1])
                    yield
                    S.op("vector", lambda e: e.tensor_scalar(out=V("k"), in0=V("sig"), scalar1=noml[:, hh:hh + 1], scalar2=oml[:, hh:hh + 1], op0=ALU.mult, op1=ALU.add),
                         reads=[K_("sig"), "noml", "oml"], writes=[K_("k")])
                    yield
                    scm = C("scanmS")[:, 0:N] if smp else C("scanm")[:, 0:N]
                    S.op("vector", lambda e: e.tensor_tensor_scan(out=V("b"), data0=scm, data1=V("logf"), initial=0.0, op0=ALU.mult, op1=ALU.add),
                         reads=[K_("logf"), "cst"], writes=[K_("b")])
                    yield
                    act(EB, V("b"), AF.Exp, [K_("b")], [("eb", par)])
                    yield
                    act(V("enb"), V("b"), AF.Exp, [K_("b")], [K_("enb")], scale=-1.0)
                    yield
                    tt("vector", qt[par][:, 0:N], V("q"), EB, ALU.mult, [K_("q"), ("eb", par)], [("qt", par)])
                    yield
                    tt("vector", V("kt"), V("k"), V("enb"), ALU.mult, [K_("k"), K_("enb")], [K_("kt")])
                    yield
                    act(ktb[par][:, 0:N], V("kt"), AF.Copy, [K_("kt")], [("ktb", par)])
                    yield
                    CL = 4 if smp else 64
                    ebl = EB.rearrange("p (c t) -> p c t", t=CL)[:, :, CL - 1:CL].to_broadcast([128, N // CL, CL])
                    S.op("vector", lambda e: e.tensor_tensor(out=kh[par][:, 0:N].rearrange("p (c t) -> p c t", t=CL), in0=V("kt").rearrange("p (c t) -> p c t", t=CL), in1=ebl, op=ALU.mult),
                         reads=[K_("kt"), ("eb", par)], writes=[("kh", par)])
                    yield
                else:
                    for nm, tsn, ta, tb_, dst, dkey in (("qs", 1, "t1", "t2", qt[par], ("qt", par)), ("ks", 3, "t3", "t4", ktb[par], ("ktb", par))):
                        tt("gpsimd", V(tb_), V(nm), tabs[tsn][:, 0:N], ALU.mult, [K_(nm) + "lo", K_(nm) + "hi", ("tab", tsn)], [K_(tb_)])
                        yield
                        tt("vector", dst[:, 0:N], V(ta), V(tb_), ALU.add, [K_(ta), K_(tb_)], [dkey])
                        yield

            def bctx(h, bi, par):
                c0, N, kind = BLOCKS[bi]
                return dict(c0=c0, N=N, is_h=h < 4, hh=h % 4, smp=kind == "s", ntile=N // 128,
                            QT=qt[par], KTB=ktb[par], VBF=vbf[par], QK=("qt", par), KK=("ktb", par), VK=("vbf", par), EBF=ebb[par])

            def gen_pre(h, bi, par):
                x = bctx(h, bi, par)
                is_h, hh, smp, ntile = x["is_h"], x["hh"], x["smp"], x["ntile"]
                QT, KTB, VBF, QK, KK, VK = x["QT"], x["KTB"], x["VBF"], x["QK"], x["KK"], x["VK"]
                ksrc, ksrc_key = (kh[par], ("kh", par)) if is_h else (ktb[par], ("ktb", par))
                A4 = B[4][:, :].rearrange("p (t c) -> p t c", t=4)
                TR = Bbf[5][:, 0:512].rearrange("p (t c) -> p t c", t=4)
                for t in range(ntile):
                    cs = slice(t * 128, t * 128 + 128)
                    S.op("tensor", lambda e, t=t, cs=cs: e.matmul(A4[:, t, :], lhsT=KTB[:, cs], rhs=QT[:, cs], start=True, stop=True),
                         reads=[KK, QK], writes=[("B", 4)])
                for t in range(ntile):
                    cs = slice(t * 128, t * 128 + 128)
                    S.op("tensor", lambda e, t=t, cs=cs: e.transpose(out=TR[:, t, :], in_=ksrc[:, cs], identity=ident[:, :]),
                         reads=[ksrc_key, "ident"], writes=[("B", 5)])
                yield
                if is_h:
                    mk = C("maskHS") if smp else C("maskH")
                else:
                    mk = C("maskRS%d" % hh) if smp else C("maskR%d" % hh)
                mkb = mk.unsqueeze(1).to_broadcast([128, ntile, 128])
                tt("vector", atm[:, 0:ntile, :], A4[:, 0:ntile, :], mkb, ALU.mult, [("B", 4), "cst"], ["atm"])
                if is_h and not smp:
                    act(khtok[:, 0:ntile, :], TR[:, 0:ntile, :], AF.Copy, [("B", 5), "cst"], ["khtok"], scale=C("rm0"))
                    act(khtok2[:, 0:ntile, :], TR[:, 0:ntile, :], AF.Copy, [("B", 5), "cst"], ["khtok2"], scale=C("rm1"))
                elif is_h:
                    act(khtok[:, 0:ntile, :], TR[:, 0:ntile, :], AF.Copy, [("B", 5)], ["khtok"])
                else:
                    kc = 4 + hh if smp else hh
                    act(khtok[:, 0:ntile, :], TR[:, 0:ntile, :], AF.Copy, [("B", 5), "cst"], ["khtok"], scale=C("ksc")[:, kc:kc + 1])
                yield
                if not smp:
                    nch = 2 if is_h else 1
                    for t in range(ntile):
                        for c in range(nch):
                            bank = 6 + c
                            ksb = khtok2 if c == 1 else khtok
                            S.op("tensor", lambda e, t=t, bank=bank, ksb=ksb: e.matmul(B[bank][:, t * 128:(t + 1) * 128], lhsT=ksb[:, t, :],
                                                                                     rhs=VBF[:, t * 128:(t + 1) * 128], start=True, stop=True),
                                 reads=["khtok", "khtok2", VK], writes=[("B", bank)])
                    yield

            def gen_chain(h, bi, par, st):
                x = bctx(h, bi, par)
                is_h, hh, smp, ntile, EBF = x["is_h"], x["hh"], x["smp"], x["ntile"], x["EBF"]
                if bi == 0:
                    st["gc"] = 0
                    st["sp"] = 0
                    S.op("gpsimd", lambda e: e.memset(Sbf[:, 0, :], 0.0), writes=[("Sbf", 0)])
                    S.op("gpsimd", lambda e: e.memset(S32[:, 0, :], 0.0), writes=[("S32", 0)])
                    yield
                st["gc0"] = st["gc"]
                if smp:
                    return
                nch = 2 if is_h else 1
                for t in range(ntile):
                    for c in range(nch):
                        bank = 6 + c
                        pc = B[bank][:, t * 128:(t + 1) * 128]
                        sp = st["sp"]
                        if is_h:
                            col = t * 128 + c * 64 + 63
                            S.op("vector", lambda e, pc=pc, col=col, sp=sp: e.scalar_tensor_tensor(out=S32[:, 1 - sp, :], in0=S32[:, sp, :], scalar=EBF[:, col:col + 1], in1=pc, op0=ALU.mult, op1=ALU.add),
                                 reads=[("S32", sp), ("eb", par), ("B", bank)], writes=[("S32", 1 - sp)])
                        else:
                            S.op("vector", lambda e, pc=pc, sp=sp: e.scalar_tensor_tensor(out=S32[:, 1 - sp, :], in0=S32[:, sp, :], scalar=decays[hh][0], in1=pc, op0=ALU.mult, op1=ALU.add),
                                 reads=[("S32", sp), ("B", bank)], writes=[("S32", 1 - sp)])
                        st["sp"] = 1 - sp
                        st["gc"] += 1
                        nslot = st["gc"] % NSB
                        yield
                        act(Sbf[:, nslot, :], S32[:, 1 - sp, :], AF.Copy, [("S32", 1 - sp)], [("Sbf", nslot)])
                        yield

            def gen_tail(h, bi, par, gi, st):
                x = bctx(h, bi, par)
                c0, N, is_h, hh, smp, ntile = x["c0"], x["N"], x["is_h"], x["hh"], x["smp"], x["ntile"]
                QT, KTB, VBF, QK, KK, VK, EBF = x["QT"], x["KTB"], x["VBF"], x["QK"], x["KK"], x["VK"], x["EBF"]
                V = lambda n: tf[n][:, 0:N]
                if not smp:
                    nch = 2 if is_h else 1
                    kk = 128 // nch
                    gc = st["gc0"]
                    for t in range(ntile):
                        tcs = slice(t * 128, (t + 1) * 128)
                        S.op("tensor", lambda e, t=t, tcs=tcs: e.matmul(B[4][:, tcs], lhsT=VBF[:, tcs], rhs=atm[:, t, :], start=True, stop=False),
                             reads=[VK, "atm"], writes=[("B", 4)])
                        for c in range(nch):
                            slot = gc % NSB
                            gc += 1
                            ccs = slice(t * 128 + c * kk, t * 128 + (c + 1) * kk)
                            S.op("tensor", lambda e, slot=slot, ccs=ccs, c=c: e.matmul(B[4][:, ccs], lhsT=Sbf[:, slot, :], rhs=QT[:, ccs], start=False, stop=(c == nch - 1)),
                                 reads=[("Sbf", slot), QK], writes=[("B", 4)])
                    if bi == 3:
                        dst = (hsp if h < 4 else rsp)[h % 4]
                        sp = st["sp"]
                        S.dma(lambda e, dst=dst, sp=sp: e.dma_start(out=dst, in_=S32[:, sp, :]), reads=[("S32", sp)])
                else:
                    s32 = st32[0]
                    s32k = ("st32", 0)
                    S.op("gpsimd", lambda e: e.tensor_copy(out=stbf[:], in_=s32[:]), reads=[s32k], writes=["stbf"])
                    S.op("tensor", lambda e: e.matmul(B[4][:, 0:128], lhsT=VBF[:, 0:128], rhs=atm[:, 0, :], start=True, stop=False),
                         reads=[VK, "atm"], writes=[("B", 4)])
                    for s_ in range(NSQ):
                        S.op("tensor", lambda e, s_=s_: e.matmul(B[4][:, 4 * s_:4 * s_ + 4], lhsT=stbf[:, s_, :], rhs=QT[:, 4 * s_:4 * s_ + 4], start=False, stop=(s_ == NSQ - 1)),
                             reads=["stbf", QK], writes=[("B", 4)])
                act(sq[:, 0:N], B[4][:, 0:N], AF.Square, [("B", 4)], ["sq"])
                S.op("tensor", lambda e: e.matmul(B[5][:, 0:N], lhsT=ones[:, :], rhs=sq[:, 0:N], start=True, stop=True), reads=["ones", "sq"], writes=[("B", 5)])
                act(V("std"), B[5][:, 0:N], AF.Ln, [("B", 5), "cst"], ["std"], scale=1.0 / 128.0, bias=C("eps"))
                act(V("rstd"), V("std"), AF.Exp, ["std"], ["rstd"], scale=-0.5)
                tt("vector", V("on"), B[4][:, 0:N], V("rstd"), ALU.mult, [("B", 4), "rstd"], ["on"])
                tt("gpsimd", mixT[:, h, c0:c0 + N], V("on"), gate[gi][:, 0:N], ALU.mult, ["on", ("gate", gi)], [("mixT", bi)])
                if smp:
                    s32 = st32[0]
                    s32k = ("st32", 0)
                    S.op("gpsimd", lambda e: e.tensor_tensor(out=khm[:, :, :], in0=khtok[:, 0:1, :].to_broadcast([128, 16, 128]),
                                                             in1=C("seqmask").unsqueeze(2).to_broadcast([128, 16, 128]), op=ALU.mult),
                         reads=["khtok", "cst"], writes=["khm"])
                    for sg_ in range(4):
                        bank = 6 + (sg_ % 2)
                        for si in range(4):
                            s_ = sg_ * 4 + si
                            S.op("tensor", lambda e, s_=s_, si=si, bank=bank: e.matmul(B[bank][:, si * 128:(si + 1) * 128], lhsT=khm[:, s_, :], rhs=VBF[:, 0:128], start=True, stop=True),
                                 reads=["khm", VK], writes=[("B", bank)])
                        sv = s32[:, sg_ * 4:(sg_ + 1) * 4, :]
                        if is_h:
                            ebs = EBF[:, 0:64].rearrange("p (s t) -> p s t", t=4)[:, sg_ * 4:(sg_ + 1) * 4, 3:4].to_broadcast([128, 4, 128])
                            S.op("vector", lambda e, sv=sv, ebs=ebs: e.tensor_tensor(out=sv, in0=sv, in1=ebs, op=ALU.mult), reads=[s32k, ("eb", par), "stbf"], writes=[s32k])
                            S.op("vector", lambda e, sv=sv, bank=bank: e.tensor_tensor(out=sv, in0=sv, in1=B[bank][:, :].rearrange("p (s v) -> p s v", s=4), op=ALU.add),
                                 reads=[s32k, ("B", bank)], writes=[s32k])
                        else:
                            S.op("vector", lambda e, sv=sv, bank=bank: e.scalar_tensor_tensor(out=sv, in0=sv, scalar=decays[hh][1], in1=B[bank][:, :].rearrange("p (s v) -> p s v", s=4), op0=ALU.mult, op1=ALU.add),
                                 reads=[s32k, ("B", bank), "stbf"], writes=[s32k])
                    for q4 in range(4):
                        dst = (hss if is_h else rss)[q4 * 4:(q4 + 1) * 4, hh, :, :].rearrange("s d v -> d s v")
                        S.dma(lambda e, dst=dst, q4=q4: e.dma_start(out=dst, in_=s32[:, q4 * 4:(q4 + 1) * 4, :]), reads=[s32k])
                    if h + 1 < 8:
                        load_states(h + 1)
                yield

            def load_states(h):
                for q4 in range(4):
                    src = (sth if h < 4 else strr)[q4 * 4:(q4 + 1) * 4, h % 4, :, :].rearrange("s d v -> d s v")
                    S.dma(lambda e, src=src, q4=q4: e.dma_start(out=st32[0][:, q4 * 4:(q4 + 1) * 4, :], in_=src), writes=[("st32", 0)])

            items = [(h, bi) for h in range(8) for bi in range(5)]
            gctr = [0]
            load_head_weights(0)
            load_states(0)

            def run_slot(gens):
                gens = [g_ for g_ in gens if g_ is not None]
                for g_ in list(gens):
                    if getattr(g_, "gi_code", None) is not None and g_.gi_code.co_name in ("gen_elem", "elem_item"):
                        try:
                            while next(g_) != "EVAC_DONE":
                                pass
                        except StopIteration:
                            gens.remove(g_)
                while gens:
                    for g_ in list(gens):
                        try:
                            next(g_)
                        except StopIteration:
                            gens.remove(g_)

            def elem_item(i):
                h, bi = items[i]
                yield from gen_elem(h, bi, i % 3, i % 3)

            cst_state = {"gc": 0, "sp": 0, "gc0": 0}
            nit = len(items)

            def seq_item(j):
                if 0 <= j - 2 < nit:
                    yield from gen_tail(items[j - 2][0], items[j - 2][1], (j - 2) % 3, (j - 2) % 3, cst_state)
                    chk("q%da" % j)
                if 0 <= j - 1 < nit:
                    yield from gen_pre(items[j - 1][0], items[j - 1][1], (j - 1) % 3)
                    chk("q%db" % j)
                    yield from gen_chain(items[j - 1][0], items[j - 1][1], (j - 1) % 3, cst_state)

            def seq_item2(j):
                if 0 <= j - 1 < nit:
                    yield from gen_pre(items[j - 1][0], items[j - 1][1], (j - 1) % 3)
                    yield from gen_chain(items[j - 1][0], items[j - 1][1], (j - 1) % 3, cst_state)

            for j in range(-1, nit + 2):
                gens = []
                if 0 <= j - 2 < nit:
                    run_slot([gen_tail(items[j - 2][0], items[j - 2][1], (j - 2) % 3, (j - 2) % 3, cst_state)])
                if 0 <= j < nit:
                    gens.append(elem_item(j))
                if j + 1 < nit:
                    gens.append(gen_proj(*items[j + 1]))
                gens.append(seq_item2(j))
                run_slot(gens)
                chk("s%d" % j)
        flush_casts()
        chk("p1")
        S.barrier()

        with contextlib.ExitStack() as p2:
            def sb2(name, shape, dt=F32):
                return sb(name, shape, dt, st=p2)

            acc = sb("acc", [128, 17, D])
            lng = sb2("lng", [128, D])
            lnbt = sb2("lnbt", [128, D])
            xst2 = [sb2("xst2_%d" % i, [128, D]) for i in range(2)]
            rr = [sb2("rr%d" % i, [128, D]) for i in range(2)]
            xb2 = [sb2("xb2_%d" % i, [128, D], BF16) for i in range(3)]
            stats = sb2("stats", [128, 2, 6])
            mv = sb2("mv", [128, 2])
            lnv = sb2("lnv", [128, 1])
            rstd = sb2("rstd2", [128, 1])

            S.dma(lambda e, lng=lng: e.dma_start(out=lng[:], in_=lnb[0]), writes=["lng"])
            S.dma(lambda e, lnbt=lnbt: e.dma_start(out=lnbt[:], in_=lnb[1]), writes=["lnbt"])

            def layer_norm(r, rows, dst, dst_key, rkey, bufs):
                stats_, mv_, lnv_, rstd_, lng_, lnbt_ = bufs
                for c in range(2):
                    S.op("vector", lambda e, c=c: e.bn_stats(out=stats_[0:rows, c, :], in_=r[0:rows, c * 512:(c + 1) * 512]), reads=[rkey], writes=["stats"])
                S.op("vector", lambda e: e.bn_aggr(out=mv_[0:rows, :], in_=stats_[0:rows, :, :].rearrange("p a b -> p (a b)")), reads=["stats"], writes=["mv"])
                act(lnv_[0:rows, :], mv_[0:rows, 1:2], AF.Ln, ["mv", "cst"], ["lnv"], bias=C("eps")[0:rows, :])
                act(rstd_[0:rows, :], lnv_[0:rows, :], AF.Exp, ["lnv"], ["rstd2"], scale=-0.5)
                S.op("vector", lambda e: e.tensor_scalar(out=r[0:rows, :], in0=r[0:rows, :], scalar1=mv_[0:rows, 0:1], scalar2=rstd_[0:rows, 0:1], op0=ALU.subtract, op1=ALU.mult),
                     reads=[rkey, "mv", "rstd2"], writes=[rkey])
                tt("gpsimd", r[0:rows, :], r[0:rows, :], lng_[0:rows, :], ALU.mult, [rkey, "lng"], [rkey])
                tt("gpsimd", dst, r[0:rows, :], lnbt_[0:rows, :], ALU.add, [rkey, "lnbt"], [dst_key])

            ln_bufs1 = (stats, mv, lnv, rstd, lng, lnbt)

            def p1b_mm(t):
                rows = 128 if t < 16 else 64
                bi = min(t // 4, 4)
                src = xp[t * 128:(t + 1) * 128, :] if t < 16 else xs
                sl = t % 2
                tc0 = t * 128
                S.dma(lambda e: e.dma_start(out=xst2[sl][0:rows, :], in_=src), writes=[("xst2", sl)])
                for half in range(2):
                    bank = 2 * sl + half
                    for hh in range(8):
                        S.op("tensor", lambda e, hh=hh, half=half, bank=bank: e.matmul(B[bank][:, :], lhsT=mixT[:, hh, tc0:tc0 + 128], rhs=wout[:, hh, half * 512:(half + 1) * 512],
                                                                                    start=(hh == 0), stop=(hh == 7)),
                             reads=[("mixT", bi), ("wout", hh)], writes=[("B", bank)])

            def p1b_post(t):
                rows = 128 if t < 16 else 64
                bi = min(t // 4, 4)
                sl = t % 2
                tc0 = t * 128
                for half in range(2):
                    bank = 2 * sl + half
                    S.op("vector", lambda e, half=half, bank=bank: e.scalar_tensor_tensor(out=rr[sl][0:rows, half * 512:(half + 1) * 512], in0=xst2[sl][0:rows, half * 512:(half + 1) * 512],
                                                                                         scalar=ALPHA, in1=B[bank][0:rows, :], op0=ALU.mult, op1=ALU.add),
                         reads=[("xst2", sl), ("B", bank)], writes=[("rr", sl)])
                layer_norm(rr[sl], rows, acc[0:rows, t, :], ("acc", t), ("rr", sl), ln_bufs1)
                if rows < 128:
                    S.op("gpsimd", lambda e: e.memset(xb2[t % 3][rows:128, :], 0.0), writes=[("xb2", t % 3)])
                act(xb2[t % 3][0:rows, :], acc[0:rows, t, :], AF.Copy, [("acc", t)], [("xb2", t % 3)])

            def p1b_tr(t):
                rows = 128 if t < 16 else 64
                transpose_tile(xb2[t % 3], rows, t * 128, 4 + t % 2, t, inT, ("inT", min(t // 4, 4)), ("xb2", t % 3))

            p1b_mm(0)
            for t in range(17):
                if t + 1 < 17:
                    p1b_mm(t + 1)
                p1b_post(t)
                if t >= 1:
                    p1b_tr(t - 1)
            p1b_tr(16)
            chk("p1b")
            S.barrier()

        with contextlib.ExitStack() as p3:
            def sb3(name, shape, dt=F32):
                return sb(name, shape, dt, st=p3)

            lng = sb3("lng3", [128, D])
            lnbt = sb3("lnbt3", [128, D])
            wgu = [sb3("wgu%d" % i, [128, 2, 8, 128], BF16) for i in range(2)]
            hT = mixT
            sg = [sb3("sg%d" % i, [128, 512]) for i in range(2)]
            yy = [sb3("yy%d" % i, [128, D]) for i in range(2)]
            stats = sb3("stats3", [128, 2, 6])
            mv = sb3("mv3", [128, 2])
            lnv = sb3("lnv3", [128, 1])
            rstd = sb3("rstd3", [128, 1])

            ln_bufs3 = (stats, mv, lnv, rstd, lng, lnbt)
            S.dma(lambda e, lng=lng: e.dma_start(out=lng[:], in_=lnb[2]), writes=["lng"])
            S.dma(lambda e, lnbt=lnbt: e.dma_start(out=lnbt[:], in_=lnb[3]), writes=["lnbt"])
            fctr = 0
            pctr = 0
            for gi, grp in enumerate(FGROUPS):
                last = gi == len(FGROUPS) - 1
                ds = gi % 2
                for fi, f in enumerate(grp):
                    load_cast(w_down[f * 128:(f + 1) * 128, :], wout[:, ds * 4 + fi, :], ("wd", ds, fi))
                for fi, f in enumerate(grp):
                    ws = fctr % 2
                    fctr += 1
                    load_cast(w_gate_v[:, :, f * 128:(f + 1) * 128], wgu[ws][:, 0, :, :], ("wgu", ws, 0), shape3=(8, 128))
                    load_cast(w_up_v[:, :, f * 128:(f + 1) * 128], wgu[ws][:, 1, :, :], ("wgu", ws, 1), shape3=(8, 128))
                    for bi, (c0, N, kind) in enumerate(BLOCKS):
                        pb = pctr % 2
                        pctr += 1
                        for which in range(2):
                            bank = 2 * pb + which
                            for k in range(8):
                                S.op("tensor", lambda e, k=k, which=which, bank=bank, c0=c0, N=N, ws=ws: e.matmul(B[bank][:, 0:N], lhsT=wgu[ws][:, which, k, :], rhs=inT[:, k, c0:c0 + N],
                                                                                                             start=(k == 0), stop=(k == 7)),
                                     reads=[("wgu", ws, which), ("inT", bi)], writes=[("B", bank)])
                        act(sg[pb][:, 0:N], B[2 * pb][:, 0:N], AF.Silu, [("B", 2 * pb)], [("sg", pb)])
                        tt("vector", hT[:, fi, c0:c0 + N], sg[pb][:, 0:N], B[2 * pb + 1][:, 0:N], ALU.mult, [("sg", pb), ("B", 2 * pb + 1)], [("hT", fi, bi)])
                for t in range(17):
                    rows = 128 if t < 16 else 64
                    bi = min(t // 4, 4)
                    tc0 = t * 128
                    sl = t % 2
                    for half in range(2):
                        bank = 4 + 2 * sl + half
                        for fi in range(len(grp)):
                            S.op("tensor", lambda e, fi=fi, half=half, bank=bank, tc0=tc0, rows=rows, ds=ds, ng=len(grp): e.matmul(B[bank][:, :], lhsT=hT[:, fi, tc0:tc0 + 128], rhs=wout[:, ds * 4 + fi, half * 512:(half + 1) * 512],
                                                                                                          start=(fi == 0), stop=(fi == ng - 1)),
                                 reads=[("hT", fi, bi), ("wd", ds, fi)], writes=[("B", bank)])
                        hs = slice(half * 512, (half + 1) * 512)
                        dst = yy[sl][0:rows, hs] if last else acc[0:rows, t, hs]
                        dkey = ("yy", sl) if last else ("acc", t)
                        if gi == 0:
                            S.op("vector", lambda e, hs=hs, bank=bank, t=t, rows=rows, dst=dst: e.scalar_tensor_tensor(out=dst, in0=acc[0:rows, t, hs], scalar=ALPHA, in1=B[bank][0:rows, :], op0=ALU.mult, op1=ALU.add),
                                 reads=[("acc", t), ("B", bank)], writes=[dkey])
                        else:
                            tt("vector", dst, acc[0:rows, t, hs], B[bank][0:rows, :], ALU.add, [("acc", t), ("B", bank)], [dkey])
                    if last:
                        layer_norm(yy[sl], rows, yy[sl][0:rows, :], ("yy", sl), ("yy", sl), ln_bufs3)
                        odst = yp[t * 128:(t + 1) * 128, :] if t < 16 else ys
                        S.dma(lambda e, odst=odst, sl=sl, rows=rows: e.dma_start(out=odst, in_=yy[sl][0:rows, :]), reads=[("yy", sl)])
        S.emit()
    return nc


_NC_CACHE = {}


def kernel(x_prompt, x_sample, state_hgrn, state_ret, w_in, lb_logits, hgrn_norm_g, ret_norm_g,
           w_out, ln1_g, ln1_b, w_gate, w_up, w_down, ln2_g, ln2_b):
    f = lambda a: np.ascontiguousarray(np.asarray(a, dtype=np.float32))
    x_prompt, x_sample, state_hgrn, state_ret = f(x_prompt), f(x_sample), f(state_hgrn), f(state_ret)
    packed, offs, ropeq, ropek, _ = _CONSTS
    if "nc" not in _NC_CACHE:
        _NC_CACHE["nc"] = build_nc()
    nc = _NC_CACHE["nc"]
    lbl = f(f(lb_logits).reshape(2, 4, 128).transpose(2, 0, 1).reshape(128, 8))
    gcol = f(np.concatenate([f(hgrn_norm_g)[0], f(ret_norm_g)[0]], 0).T)
    lnb = f(np.stack([np.broadcast_to(f(a)[0], (128, D)) for a in (ln1_g, ln1_b, ln2_g, ln2_b)]))
    shared = dict(w_in=f(w_in)[0], w_out=f(w_out)[0], w_gate=f(w_gate)[0], w_up=f(w_up)[0], w_down=f(w_down)[0],
                  lbl=lbl, gcol=gcol, lnb=lnb, cst=packed, ropeq=ropeq, ropek=ropek)
    in_maps = []
    for c in range(8):
        m = dict(shared)
        m["xp"] = x_prompt[c]
        m["xs"] = f(x_sample[16 * c:16 * c + 16].reshape(NS, D))
        m["sth"] = f(state_hgrn[0, 16 * c:16 * c + 16])
        m["str"] = f(state_ret[0, 16 * c:16 * c + 16])
        in_maps.append(m)
    res = run_bass_kernel_spmd(nc, in_maps, core_ids=list(range(8)))
    r = res.results
    y_prompt = np.stack([r[c]["yp"] for c in range(8)], 0)
    y_sample = np.concatenate([r[c]["ys"].reshape(16, 4, D) for c in range(8)], 0)
    hsp = np.stack([r[c]["hsp"] for c in range(8)], 0)[None]
    rsp = np.stack([r[c]["rsp"] for c in range(8)], 0)[None]
    hss = np.concatenate([r[c]["hss"] for c in range(8)], 0)[None]
    rss = np.concatenate([r[c]["rss"] for c in range(8)], 0)[None]
    return (y_prompt.astype(np.float32), y_sample.astype(np.float32), hsp.astype(np.float32), rsp.astype(np.float32),
            hss.astype(np.float32), rss.astype(np.float32))
```
